# Optimizing a Trainium2 kernel written in Bass

```python
import jax, jax.numpy as jnp
from jax import lax
import numpy as np

D_MODEL = 1024
BATCH = 1
SEQ = 16384
DEPTH = 1
DEC_BATCH = 8
DEC_SEQ = 16
PAST_LEN = 1024

CHUNK = 64
LEFT_CHUNKS = 8
BAND_CHUNKS = LEFT_CHUNKS + 1
BAND_PAST = LEFT_CHUNKS * CHUNK
D_MIX = D_MODEL
D_ATTN = D_MIX // 2
D_CONV = D_MIX - D_ATTN
ATT_HEADS = 8
ATT_HEAD_DIM = D_ATTN // ATT_HEADS
REL_CLIP = 128
CONV_WIDTH = 31
CONV_GROUPS = 8
MEM_TOKENS = 256
MEM_HEADS = 4
MEM_HEAD_DIM = D_MODEL // MEM_HEADS
D_FF = 4 * D_MODEL
EPS = 1e-6
NEG_INF = -1e30

kernel_name = "hybrid_chunk_band_attn_conformer_conv_step"


def rms_norm(x, g):
    x32 = x.astype(jnp.float32)
    y = x32 * lax.rsqrt(jnp.mean(x32 * x32, axis=-1, keepdims=True) + EPS)
    return (y * g.astype(jnp.float32)).astype(x.dtype)


def group_layer_norm(u, g, b):
    shp = u.shape
    u32 = u.astype(jnp.float32).reshape(shp[:-1] + (CONV_GROUPS, D_CONV // CONV_GROUPS))
    mu = jnp.mean(u32, axis=-1, keepdims=True)
    var = jnp.mean(jnp.square(u32 - mu), axis=-1, keepdims=True)
    y = ((u32 - mu) * lax.rsqrt(var + EPS)).reshape(shp)
    return (y * g.astype(jnp.float32) + b.astype(jnp.float32)).astype(u.dtype)


def rel_bias(table, q_off, k_off):
    d = q_off[:, None] - k_off[None, :]
    idx = jnp.clip(d, -REL_CLIP, REL_CLIP) + REL_CLIP
    return table[:, idx]


def in_proj(xn, w_in):
    B, T, _ = xn.shape
    z = xn @ w_in
    q, k, v, ga, gb = jnp.split(z, [D_ATTN, 2 * D_ATTN, 3 * D_ATTN, 3 * D_ATTN + D_CONV], axis=-1)
    heads = lambda t: t.reshape(B, T, ATT_HEADS, ATT_HEAD_DIM)
    u = ga * jax.nn.sigmoid(gb)
    return heads(q), heads(k), heads(v), u


def band_attn_prompt(q, k, v, table):
    B, S, H, Dh = q.shape
    NC = S // CHUNK
    pad = ((0, 0), (BAND_PAST, 0), (0, 0), (0, 0))
    kp = jnp.pad(k, pad)
    vp = jnp.pad(v, pad)

    def band(t):
        return jnp.stack(
            [t[:, j * CHUNK: j * CHUNK + S].reshape(B, NC, CHUNK, H, Dh) for j in range(BAND_CHUNKS)],
            axis=2).reshape(B, NC, BAND_CHUNKS * CHUNK, H, Dh)

    kb, vb = band(kp), band(vp)
    qc = q.reshape(B, NC, CHUNK, H, Dh)
    s = jnp.einsum('bnqhd,bnkhd->bnhqk', qc, kb).astype(jnp.float32) * (Dh ** -0.5)
    q_off = jnp.arange(CHUNK) + BAND_PAST
    k_off = jnp.arange(BAND_CHUNKS * CHUNK)
    s = s + rel_bias(table, q_off, k_off).astype(jnp.float32)[None, None]
    chunk_valid = (jnp.arange(NC)[:, None] + jnp.arange(BAND_CHUNKS)[None, :]) >= LEFT_CHUNKS
    key_valid = jnp.repeat(chunk_valid, CHUNK, axis=1)
    s = jnp.where(key_valid[None, :, None, None, :], s, NEG_INF)
    p = jax.nn.softmax(s, axis=-1).astype(v.dtype)
    o = jnp.einsum('bnhqk,bnkhd->bnqhd', p, vb)
    return o.reshape(B, S, H * Dh)


def band_attn_sample(q, k_all, v_all, table):
    B, T, H, Dh = q.shape
    L = k_all.shape[1]
    past = L - T
    s = jnp.einsum('bqhd,bkhd->bhqk', q, k_all).astype(jnp.float32) * (Dh ** -0.5)
    s = s + rel_bias(table, jnp.arange(T) + past, jnp.arange(L)).astype(jnp.float32)[None]
    p = jax.nn.softmax(s, axis=-1).astype(v_all.dtype)
    o = jnp.einsum('bhqk,bkhd->bqhd', p, v_all)
    return o.reshape(B, T, H * Dh)


def conv_branch(u_ext, w_dw, b_dw, ln_g, ln_b):
    c = lax.conv_general_dilated(
        u_ext, w_dw[:, None, :], window_strides=(1,), padding='VALID',
        dimension_numbers=('NWC', 'WIO', 'NWC'), feature_group_count=D_CONV) + b_dw
    return jax.nn.silu(group_layer_norm(c, ln_g, ln_b))


def merge_out(a, c, g_attn_out, g_conv_out, w_out):
    return jnp.concatenate([rms_norm(a, g_attn_out), rms_norm(c, g_conv_out)], axis=-1) @ w_out


def mem_kv(mem, g_mem, wk, wv):
    B, M, _ = mem.shape
    mn = rms_norm(mem, g_mem)
    return ((mn @ wk).reshape(B, M, MEM_HEADS, MEM_HEAD_DIM),
            (mn @ wv).reshape(B, M, MEM_HEADS, MEM_HEAD_DIM))


def mem_attn(hn, mk, mv, wq, wo):
    B, T, _ = hn.shape
    q = (hn @ wq).reshape(B, T, MEM_HEADS, MEM_HEAD_DIM)
    s = jnp.einsum('bqhd,bmhd->bhqm', q, mk).astype(jnp.float32) * (MEM_HEAD_DIM ** -0.5)
    p = jax.nn.softmax(s, axis=-1).astype(mv.dtype)
    o = jnp.einsum('bhqm,bmhd->bqhd', p, mv).reshape(B, T, MEM_HEADS * MEM_HEAD_DIM)
    return o @ wo


def sq_relu_mlp(hn, w_up, w_down):
    return jnp.square(jax.nn.relu(hn @ w_up)) @ w_down


def setup_inputs(seed: int = 0) -> dict:
    key = jax.random.key(seed)
    ks = iter(jax.random.split(key, 40))
    nrm = lambda shape, scale: jax.random.normal(next(ks), shape, jnp.float32) * scale
    gain = lambda n: 1.0 + nrm((DEPTH, n), 0.02)
    keep = min(BAND_PAST, PAST_LEN)
    return {
        "x_prompt": nrm((BATCH, SEQ, D_MODEL), 1.0),
        "x_sample": nrm((DEC_BATCH, DEC_SEQ, D_MODEL), 1.0),
        "mem_prompt": nrm((BATCH, MEM_TOKENS, D_MODEL), 1.0),
        "cache_attn_k": nrm((DEPTH, DEC_BATCH, keep, ATT_HEADS, ATT_HEAD_DIM), 1.0),
        "cache_attn_v": nrm((DEPTH, DEC_BATCH, keep, ATT_HEADS, ATT_HEAD_DIM), 1.0),
        "cache_conv": nrm((DEPTH, DEC_BATCH, CONV_WIDTH - 1, D_CONV), 1.0),
        "cache_mem_k": nrm((DEPTH, DEC_BATCH, MEM_TOKENS, MEM_HEADS, MEM_HEAD_DIM), 1.0),
        "cache_mem_v": nrm((DEPTH, DEC_BATCH, MEM_TOKENS, MEM_HEADS, MEM_HEAD_DIM), 1.0),
        "g_mix_pre": gain(D_MODEL),
        "w_in": nrm((DEPTH, D_MODEL, 3 * D_ATTN + 2 * D_CONV), D_MODEL ** -0.5),
        "att_rel_bias": nrm((DEPTH, ATT_HEADS, 2 * REL_CLIP + 1), 0.1),
        "w_dw": nrm((DEPTH, CONV_WIDTH, D_CONV), CONV_WIDTH ** -0.5),
        "b_dw": nrm((DEPTH, D_CONV), 0.02),
        "conv_ln_g": gain(D_CONV),
        "conv_ln_b": nrm((DEPTH, D_CONV), 0.02),
        "g_attn_out": gain(D_ATTN),
        "g_conv_out": gain(D_CONV),
        "w_out": nrm((DEPTH, D_MIX, D_MODEL), D_MIX ** -0.5),
        "g_mix_post": gain(D_MODEL),
        "g_mem_pre": gain(D_MODEL),
        "g_mem_kv": gain(D_MODEL),
        "w_mem_q": nrm((DEPTH, D_MODEL, MEM_HEADS * MEM_HEAD_DIM), D_MODEL ** -0.5),
        "w_mem_k": nrm((DEPTH, D_MODEL, MEM_HEADS * MEM_HEAD_DIM), D_MODEL ** -0.5),
        "w_mem_v": nrm((DEPTH, D_MODEL, MEM_HEADS * MEM_HEAD_DIM), D_MODEL ** -0.5),
        "w_mem_o": nrm((DEPTH, MEM_HEADS * MEM_HEAD_DIM, D_MODEL), D_MODEL ** -0.5),
        "g_mem_post": gain(D_MODEL),
        "g_ffn_pre": gain(D_MODEL),
        "w_ffn_up": nrm((DEPTH, D_MODEL, D_FF), D_MODEL ** -0.5),
        "w_ffn_down": nrm((DEPTH, D_FF, D_MODEL), D_FF ** -0.5),
        "g_ffn_post": gain(D_MODEL),
    }


def reference(x_prompt, x_sample, mem_prompt, cache_attn_k, cache_attn_v, cache_conv, cache_mem_k, cache_mem_v,
              g_mix_pre, w_in, att_rel_bias, w_dw, b_dw, conv_ln_g, conv_ln_b, g_attn_out, g_conv_out, w_out,
              g_mix_post, g_mem_pre, g_mem_kv, w_mem_q, w_mem_k, w_mem_v, w_mem_o, g_mem_post,
              g_ffn_pre, w_ffn_up, w_ffn_down, g_ffn_post):
    xp, xs = x_prompt, x_sample
    S = xp.shape[1]
    T = xs.shape[1]
    keep_p = min(BAND_PAST, S)
    p_k, p_v, p_c, p_mk, p_mv, s_k, s_v, s_c = [], [], [], [], [], [], [], []
    for l in range(DEPTH):
        q, k, v, u = in_proj(rms_norm(xp, g_mix_pre[l]), w_in[l])
        a = band_attn_prompt(q, k, v, att_rel_bias[l])
        u_ext = jnp.pad(u, ((0, 0), (CONV_WIDTH - 1, 0), (0, 0)))
        c = conv_branch(u_ext, w_dw[l], b_dw[l], conv_ln_g[l], conv_ln_b[l])
        xp = xp + rms_norm(merge_out(a, c, g_attn_out[l], g_conv_out[l], w_out[l]), g_mix_post[l])
        p_k.append(k[:, S - keep_p:])
        p_v.append(v[:, S - keep_p:])
        p_c.append(u[:, S - (CONV_WIDTH - 1):])
        q, k, v, u = in_proj(rms_norm(xs, g_mix_pre[l]), w_in[l])
        k_all = jnp.concatenate([cache_attn_k[l], k], axis=1)
        v_all = jnp.concatenate([cache_attn_v[l], v], axis=1)
        a = band_attn_sample(q, k_all, v_all, att_rel_bias[l])
        u_ext = jnp.concatenate([cache_conv[l], u], axis=1)
        c = conv_branch(u_ext, w_dw[l], b_dw[l], conv_ln_g[l], conv_ln_b[l])
        xs = xs + rms_norm(merge_out(a, c, g_attn_out[l], g_conv_out[l], w_out[l]), g_mix_post[l])
        s_k.append(k_all[:, T:])
        s_v.append(v_all[:, T:])
        s_c.append(u_ext[:, T:])
        mk, mv = mem_kv(mem_prompt, g_mem_kv[l], w_mem_k[l], w_mem_v[l])
        p_mk.append(mk)
        p_mv.append(mv)
        xp = xp + rms_norm(mem_attn(rms_norm(xp, g_mem_pre[l]), mk, mv, w_mem_q[l], w_mem_o[l]), g_mem_post[l])
        xs = xs + rms_norm(mem_attn(rms_norm(xs, g_mem_pre[l]), cache_mem_k[l], cache_mem_v[l],
                                    w_mem_q[l], w_mem_o[l]), g_mem_post[l])
        xp = xp + rms_norm(sq_relu_mlp(rms_norm(xp, g_ffn_pre[l]), w_ffn_up[l], w_ffn_down[l]), g_ffn_post[l])
        xs = xs + rms_norm(sq_relu_mlp(rms_norm(xs, g_ffn_pre[l]), w_ffn_up[l], w_ffn_down[l]), g_ffn_post[l])
    return (xp, xs, jnp.stack(p_k), jnp.stack(p_v), jnp.stack(p_c), jnp.stack(p_mk), jnp.stack(p_mv),
            jnp.stack(s_k), jnp.stack(s_v), jnp.stack(s_c))
```

```python
import numpy as np
from contextlib import ExitStack
import concourse.bass as bass
import concourse.mybir as mybir
from concourse.bass_utils import run_bass_kernel_spmd

F32 = mybir.dt.float32
BF16 = mybir.dt.bfloat16
AF = mybir.ActivationFunctionType
ALU = mybir.AluOpType
AX = mybir.AxisListType

NCORES = 8
S = 16384
TOK = S // NCORES
D = 1024
EPS = 1e-6
NSLOT = 4
import os
STOP = os.environ.get('MK_STOP', '')


class Buf:
    def __init__(self, name, t, excl=False):
        self.name = name
        self.t = t
        self.excl = excl
        self.writers = {}
        self.readers = {}
        self.dsem = None
        self.dcount = 0


class K:
    def __init__(self, nc, st):
        self.nc = nc
        self.st = st
        self.engs = {"pe": nc.tensor, "act": nc.scalar, "dve": nc.vector, "pool": nc.gpsimd, "sp": nc.sync}
        self.sems = {}
        self.cnt = {}
        for e in ("pe", "act", "dve"):
            self.sems[e] = st.enter_context(nc.semaphore("c_" + e))
            self.cnt[e] = 0
        self.seen = {e: {} for e in self.engs}
        self.bufs = []
        self.d2d = st.enter_context(nc.semaphore("d2d"))
        self.d2d_cnt = 0
        self.zip_state = None

    def _yield(self):
        if self.zip_state is not None:
            self.zip_state["yield"]()

    def sb(self, name, shape, dt):
        b = Buf(name, self.st.enter_context(self.nc.sbuf_tensor("s_" + name, shape, dt)))
        self.bufs.append(b)
        return b

    def ps(self, name, shape, dt):
        b = Buf(name, self.st.enter_context(self.nc.psum_tensor("p_" + name, shape, dt)), excl=True)
        self.bufs.append(b)
        return b

    def _waits(self, eng, R, W):
        need = {}

        def add(d):
            for k, v in d.items():
                if need.get(k, 0) < v:
                    need[k] = v
        for b in R:
            add(b.writers)
            if b.excl:
                add(b.readers)
        for b in W:
            add(b.writers)
            add(b.readers)
        E = self.engs[eng]
        for k, v in need.items():
            if k == "pe" and eng == "pe":
                continue
            if self.seen[eng].get(k, 0) < v:
                E.wait_ge(self.sems[k], v)
                self.seen[eng][k] = v

    def op(self, eng, fn, R=(), W=(), sig=True):
        self._waits(eng, R, W)
        ins = fn(self.engs[eng])
        if sig:
            self.cnt[eng] += 1
            ins.then_inc(self.sems[eng], 1)
            val = self.cnt[eng]
        else:
            val = self.cnt[eng] + 1
        for b in W:
            b.writers = {eng: val}
            b.readers = {}
        for b in R:
            if b in W:
                continue
            if b.readers.get(eng, 0) < val:
                b.readers[eng] = val
        if sig:
            self._yield()
        return ins

    def dma(self, q, out, in_, R=(), W=()):
        self._waits(q, R, W)
        ins = self.engs[q].dma_start(out=out, in_=in_)
        b = W[0] if W else (R[0] if R else None)
        if b is None:
            self.d2d_cnt += 16
            ins.then_inc(self.d2d, 16)
            return ins
        if b.dsem is None:
            b.dsem = self.st.enter_context(self.nc.semaphore("d_" + b.name))
            self.sems["d_" + b.name] = b.dsem
        b.dcount += 16
        ins.then_inc(b.dsem, 16)
        key = "d_" + b.name
        for w in W:
            w.writers = {key: b.dcount}
            w.readers = {}
        for r in R:
            r.readers[key] = b.dcount
        return ins

    def finish(self):
        sp = self.engs["sp"]
        for b in self.bufs:
            if b.dsem is not None:
                sp.wait_ge(b.dsem, b.dcount)
        if self.d2d_cnt:
            sp.wait_ge(self.d2d, self.d2d_cnt)
        for e in ("pe", "act", "dve"):
            sp.wait_ge(self.sems[e], self.cnt[e])


class Grp:
    pass


import threading


def zip_run(k, fns, weights=None):
    n = len(fns)
    if n == 1:
        fns[0]()
        return
    assert k.zip_state is None, "nested zip_run not supported"
    sems_ = [threading.Semaphore(0) for _ in range(n)]
    main_sem = threading.Semaphore(0)
    alive = [True] * n
    errs = []

    def pass_baton(i):
        for d in range(1, n + 1):
            j = (i + d) % n
            if alive[j]:
                return j
        return None

    wts = list(weights) if weights else [1] * n
    cnt = [0] * n

    def yield_point():
        i = k.zip_state["cur"]
        cnt[i] += 1
        if cnt[i] < wts[i]:
            return
        cnt[i] = 0
        j = pass_baton(i)
        if j is None or j == i:
            return
        k.zip_state["cur"] = j
        sems_[j].release()
        sems_[i].acquire()

    def runner(i):
        sems_[i].acquire()
        try:
            fns[i]()
        except BaseException as e:
            errs.append(e)
        alive[i] = False
        j = pass_baton(i)
        if j is None:
            main_sem.release()
        else:
            k.zip_state["cur"] = j
            sems_[j].release()

    k.zip_state = {"cur": 0, "yield": yield_point}
    ths = [threading.Thread(target=runner, args=(i,)) for i in range(n)]
    for t in ths:
        t.start()
    sems_[0].release()
    main_sem.acquire()
    for t in ths:
        t.join()
    k.zip_state = None
    if errs:
        raise errs[0]


def build():
    nc = bass.Bass("TRN2", target_bir_lowering=False)
    dt_in = lambda n, s: nc.dram_tensor(n, s, F32, kind="ExternalInput")
    dt_out = lambda n, s: nc.dram_tensor(n, s, F32, kind="ExternalOutput")
    xh = dt_in("xh", [TOK + 512, D]).ap()
    xs = dt_in("xs", [16, D]).ap()
    mem = dt_in("mem", [256, D]).ap()
    ck = dt_in("ck", [512, 512]).ap()
    cv = dt_in("cv", [512, 512]).ap()
    cconv = dt_in("cconv", [30, 512]).ap()
    cmk = dt_in("cmk", [256, D]).ap()
    cmv = dt_in("cmv", [256, D]).ap()
    w_in = dt_in("w_in", [D, 2560]).ap()
    w_out = dt_in("w_out", [D, D]).ap()
    wq = dt_in("wq", [D, D]).ap()
    wk = dt_in("wk", [D, D]).ap()
    wv = dt_in("wv", [D, D]).ap()
    wo = dt_in("wo", [D, D]).ap()
    wup = dt_in("wup", [D, 4096]).ap()
    wdn = dt_in("wdn", [4096, D]).ap()
    gfm_d = dt_in("gfm", [128, 4, 8]).ap()
    gac_d = dt_in("gac", [128, 2, 4]).ap()
    wdw_d = dt_in("wdw", [128, 4, 31]).ap()
    bdw_d = dt_in("bdw", [128, 4]).ap()
    lng_d = dt_in("lng", [1, 512]).ap()
    lnb_d = dt_in("lnb", [1, 512]).ap()
    gpost_d = dt_in("gpost", [3, D]).ap()
    tabp_h = dt_in("tabp", [8, 768])
    valid_d = dt_in("valid", [128, 20]).ap()
    ident_d = dt_in("ident", [128, 128]).ap()
    anti_d = dt_in("anti", [128, 128]).ap()

    y = dt_out("y", [TOK, D]).ap()
    ys = dt_out("ys", [16, D]).ap()
    kout = dt_out("kout", [512, 512]).ap()
    vout = dt_out("vout", [512, 512]).ap()
    uout = dt_out("uout", [32, 512]).ap()
    mko = dt_out("mko", [256, D]).ap()
    mvo = dt_out("mvo", [256, D]).ap()
    sk = dt_out("sk", [512, 512]).ap()
    sv = dt_out("sv", [512, 512]).ap()
    sc = dt_out("sc", [30, 512]).ap()

    with ExitStack() as st:
        k = K(nc, st)
        idf = k.sb("idf", [128, 128], F32)
        idb = k.sb("idb", [128, 128], BF16)
        onesb = k.sb("onesb", [128, 128], BF16)
        jb = k.sb("jb", [128, 128], BF16)
        gfm = k.sb("gfm", [128, 4, 8], F32)
        gac = k.sb("gac", [128, 2, 4], F32)
        wdw = k.sb("wdw", [128, 4, 31], F32)
        bdw = k.sb("bdw", [128, 4], F32)
        lng = k.sb("lng", [128, 512], F32)
        lnb = k.sb("lnb", [128, 512], F32)
        gbc0 = k.sb("gbc0", [128, D], F32)
        valid = k.sb("valid", [128, 20], F32)
        EB = k.sb("EB", [128, 5, 8, 128], BF16)
        ring = [k.sb("ring%d" % i, [128, 4096], BF16) for i in range(NSLOT)]
        KR = k.sb("KR", [128, 4, 8, 128], BF16)
        VR = k.sb("VR", [128, 8, 8, 65], BF16)
        KRs = k.sb("KRs", [128, 4, 5, 128], BF16)
        VRs = k.sb("VRs", [128, 5, 8, 65], BF16)
        mkT = k.sb("mkT", [128, 8, 256], BF16)
        mv = k.sb("mv", [128, 2, D], BF16)
        mkTs = k.sb("mkTs", [128, 8, 256], BF16)
        mvs = k.sb("mvs", [128, 2, D], BF16)
        xt = [k.sb("xt%d" % i, [128, D], F32) for i in range(4)]
        xst = k.sb("xst", [16, D], F32)
        W1 = k.sb("W1", [128, 8, 512], BF16)
        W2 = k.sb("W2", [128, 8, 512], BF16)
        W3 = k.sb("W3", [128, 8, 512], BF16)
        W4 = k.sb("W4", [128, 5120], BF16)
        W1s = k.sb("W1s", [128, 8, 16], BF16)
        W2s = k.sb("W2s", [128, 8, 16], BF16)
        W3s = k.sb("W3s", [128, 8, 16], BF16)
        W4s = k.sb("W4s", [128, 640], BF16)
        acc = [k.sb("acc%d" % i, [128, D], F32) for i in range(4)]
        accs = k.sb("accs", [16, D], F32)
        uT = k.sb("uT", [128, 4, 542], BF16)
        uTs = k.sb("uTs", [128, 4, 46], F32)
        u32 = k.sb("u32", [128, 4, 32], F32)
        u32s = k.sb("u32s", [128, 4, 16], F32)
        NDG = 4
        dg = [k.sb("dg%d" % i, [128, 128], BF16) for i in range(NDG)]
        dg_i = [0]
        cb = [k.sb("cb%d" % i, [128, 512], F32) for i in range(4)]
        cbs = [k.sb("cbs%d" % i, [128, 16], F32) for i in range(4)]
        tA = k.sb("tA", [128, D], F32)
        tE = [k.sb("tE%d" % i, [128, D], BF16) for i in range(2)]
        te_i = [0]
        tC = k.sb("tC", [128, 512], F32)
        tD = k.sb("tD", [128, 512], F32)
        xnb = k.sb("xnb", [128, D], BF16)
        a32 = k.sb("a32", [128, 512], F32)
        a32b = k.sb("a32b", [128, 512], F32)
        st_n = k.sb("st_n", [128, 4], F32)
        st_n2 = k.sb("st_n2", [128, 4], F32)
        st_n3 = k.sb("st_n3", [128, 4], F32)
        st_p = k.sb("st_p", [128, 4], F32)
        st_p2 = k.sb("st_p2", [128, 4], F32)
        st_n4 = k.sb("st_n4", [128, 4], F32)
        st_a = k.sb("st_a", [128, 8], F32)
        st_c = [k.sb("st_c%d" % i, [128, 32], F32) for i in range(2)]
        sbf = [k.sb("sbf%d" % i, [128, 512], BF16) for i in range(2)]
        tD2 = [tD, k.sb("tDb", [128, 512], F32)]
        tC2 = tC
        stage = tD2[1]
        stage2 = a32
        PA = [k.ps("PA%d" % i, [128, D], F32) for i in range(2)]
        PB = [k.ps("PB%d" % i, [128, 512], F32) for i in range(2)]
        PO = k.ps("PO", [128, 512], F32)
        PT = k.ps("PT", [128, 8, 128], BF16)

        pa_i = [0]
        pb_i = [0]

        def nextPA():
            pa_i[0] ^= 1
            return PA[pa_i[0]]

        def nextPB():
            pb_i[0] ^= 1
            return PB[pb_i[0]]

        sched = []

        def kview(w, c0):
            return w.rearrange("(kc p) c -> p kc c", p=128)[:, :, c0:c0 + 512]

        def dview(c):
            return wdn[c * 512:(c + 1) * 512, :].rearrange("(j p) c -> p j c", p=128)

        def sched_pass(kind):
            if kind == "pre":
                for nm, w in (("wk", wk), ("wv", wv)):
                    for b in range(2):
                        sched.append((nm + str(b), "k", kview(w, b * 512)))
                return
            if kind == "halo":
                for nm, c0 in (("k", 512), ("v", 1024), ("gb", 2048), ("ga", 1536)):
                    sched.append((nm, "k", kview(w_in, c0)))
                return
            for nm, c0 in (("q", 0), ("k", 512), ("v", 1024), ("gb", 2048), ("ga", 1536)):
                sched.append((nm, "k", kview(w_in, c0)))
            for nm, w in (("wout", w_out), ("wq", wq), ("wo", wo)):
                if (STOP in ("mixer", "inproj", "attn", "conv") and nm != "wout"):
                    continue
                for b in range(2):
                    sched.append((nm + str(b), "k", kview(w, b * 512)))
            if STOP in ("inproj", "attn", "conv"):
                del sched[-2:]
            if STOP:
                return
            sched.append(("up0", "k", kview(wup, 0)))
            for c in range(8):
                if c + 1 < 8:
                    sched.append(("up%d" % (c + 1), "k", kview(wup, (c + 1) * 512)))
                sched.append(("dn%d" % c, "d", dview(c)))

        sched_pass("pre")
        sched_pass("halo")
        for _ in range(4):
            sched_pass("full")
        issued = [0]
        popped = [0]

        def pop(name, keep_prev=False):
            i = popped[0]
            assert sched[i][0] == name, (sched[i][0], name)
            lim = i + NSLOT - (1 if keep_prev else 0)
            while issued[0] < len(sched) and issued[0] < lim:
                j = issued[0]
                slot = ring[j % NSLOT]
                _, kind, src = sched[j]
                if kind == "k":
                    dst = slot.t[:].rearrange("p (kc c) -> p kc c", kc=8)
                else:
                    dst = slot.t[:].rearrange("p (j c) -> p j c", j=4)
                k.dma("pool", dst, src, W=[slot])
                issued[0] += 1
            popped[0] += 1
            slot = ring[i % NSLOT]
            if sched[i][1] == "k":
                return slot, slot.t[:].rearrange("p (kc c) -> p kc c", kc=8)
            return slot, slot.t[:].rearrange("p (j c) -> p j c", j=4)

        def rstd_from_ss(sb_, r, col, scale):
            k.op("act", lambda e: e.activation(out=sb_.t[0:r, col + 1:col + 2], in_=sb_.t[0:r, col:col + 1],
                                               func=AF.Ln, scale=scale, bias=epsb.t[0:r, 0:1]), R=[sb_, epsb], W=[sb_])
            k.op("act", lambda e: e.activation(out=sb_.t[0:r, col + 1:col + 2], in_=sb_.t[0:r, col + 1:col + 2],
                                               func=AF.Exp, scale=-0.5), R=[sb_], W=[sb_])

        def norm_pre(src_ap, src_buf, r, width, xn_buf, sb_=None):
            sb_ = sb_ or st_n
            k.op("act", lambda e: e.activation(out=xn_buf.t[0:r, 0:width], in_=src_ap, func=AF.Square,
                                               accum_out=sb_.t[0:r, 0:1]), R=[src_buf], W=[xn_buf, sb_])
            rstd_from_ss(sb_, r, 0, 1.0 / width)
            k.op("act", lambda e: e.activation(out=xn_buf.t[0:r, 0:width], in_=src_ap, func=AF.Identity,
                                               scale=sb_.t[0:r, 1:2]), R=[src_buf, sb_], W=[xn_buf])

        def norm_T(xn_buf, r, width, gain_ap, dst_ap, dst_buf):
            nk = width // 128
            for kc in range(nk):
                k.op("pe", lambda e: e.transpose(PT.t[:, kc, 0:r], xn_buf.t[0:r, kc * 128:(kc + 1) * 128],
                                                 idb.t[0:r, 0:r]), R=[xn_buf, idb], W=[PT], sig=(kc == nk - 1))
            k.op("dve", lambda e: e.tensor_tensor(out=dst_ap, in0=PT.t[:, 0:nk, 0:r],
                                                  in1=gain_ap.unsqueeze(2).to_broadcast([128, nk, r]),
                                                  op=ALU.mult), R=[PT, gfm, gac], W=[dst_buf])

        def norm_to_T(src_ap, src_buf, r, width, gain_ap, dst_ap, dst_buf, xn_buf=None, sb_=None):
            xn_buf = xn_buf or xnb
            norm_pre(src_ap, src_buf, r, width, xn_buf, sb_)
            norm_T(xn_buf, r, width, gain_ap, dst_ap, dst_buf)

        def post_norm_residual(ps_ap, ps_buf, r, gb_buf, x_ap, x_buf, alt=False):
            if alt:
                jb_ = tE[1]
                k.op("act", lambda e: e.activation(out=jb_.t[0:r, :], in_=ps_ap, func=AF.Square,
                                                   accum_out=st_p2.t[0:r, 0:1]), R=[ps_buf], W=[jb_, st_p2])
                rstd_from_ss(st_p2, r, 0, 1.0 / D)
                k.op("dve", lambda e: e.scalar_tensor_tensor(out=ps_ap, in0=ps_ap, scalar=st_p2.t[0:r, 1:2],
                                                             in1=gb_buf.t[0:r, :], op0=ALU.mult, op1=ALU.mult),
                     R=[st_p2, gb_buf], W=[ps_buf])
                k.op("dve", lambda e: e.tensor_tensor(out=x_ap, in0=x_ap, in1=ps_ap, op=ALU.add),
                     R=[ps_buf, x_buf], W=[x_buf])
                return
            k.op("act", lambda e: e.activation(out=tA.t[0:r, :], in_=ps_ap, func=AF.Square,
                                               accum_out=st_p.t[0:r, 0:1]), R=[ps_buf], W=[tA, st_p])
            rstd_from_ss(st_p, r, 0, 1.0 / D)
            k.op("dve", lambda e: e.scalar_tensor_tensor(out=tA.t[0:r, :], in0=ps_ap, scalar=st_p.t[0:r, 1:2],
                                                         in1=gb_buf.t[0:r, :], op0=ALU.mult, op1=ALU.mult),
                 R=[ps_buf, st_p, gb_buf], W=[tA])
            k.op("dve", lambda e: e.tensor_tensor(out=x_ap, in0=x_ap, in1=tA.t[0:r, :], op=ALU.add),
                 R=[tA, x_buf], W=[x_buf])

        def fm_proj(blk_buf, blk, src, src_buf, n, oc, c0=0):
            pb = nextPB()
            for kc in range(8):
                k.op("pe", lambda e: e.matmul(pb.t[:, 0:n], lhsT=blk[:, kc, oc * 128:(oc + 1) * 128],
                                              rhs=src[:, kc, c0:c0 + n], start=(kc == 0), stop=(kc == 7)),
                     R=[blk_buf, src_buf], W=[pb], sig=(kc == 7))
            return pb

        def tm_proj(blk_buf, blk, src, src_buf, c0, r, pa, half):
            for kc in range(8):
                k.op("pe", lambda e: e.matmul(pa.t[0:r, half * 512:(half + 1) * 512], lhsT=src[:, kc, c0:c0 + r],
                                              rhs=blk[:, kc, :], start=(kc == 0), stop=(kc == 7)),
                     R=[blk_buf, src_buf], W=[pa], sig=(kc == 7))

        epsb = k.sb("epsb", [128, 1], F32)
        k.op("dve", lambda e: e.memset(epsb.t[:], EPS), W=[epsb])
        oneb = k.sb("oneb", [128, 1], F32)
        k.op("dve", lambda e: e.memset(oneb.t[:], 1.0), W=[oneb])
        k.dma("sp", idf.t[:], ident_d, W=[idf])
        k.dma("sp", gfm.t[:], gfm_d, W=[gfm])
        k.dma("sp", gac.t[:], gac_d, W=[gac])
        k.dma("sp", wdw.t[:], wdw_d, W=[wdw])
        k.dma("sp", bdw.t[:], bdw_d, W=[bdw])
        k.dma("sp", lng.t[:], lng_d.partition_broadcast(128), W=[lng])
        k.dma("sp", lnb.t[:], lnb_d.partition_broadcast(128), W=[lnb])
        k.dma("sp", valid.t[:], valid_d, W=[valid])
        k.op("dve", lambda e: e.tensor_copy(out=idb.t[:], in_=idf.t[:]), R=[idf], W=[idb])
        k.op("dve", lambda e: e.memset(onesb.t[:], 1.0), W=[onesb])
        k.op("dve", lambda e: e.memset(uT.t[:], 0.0), W=[uT])
        k.op("dve", lambda e: e.memset(VRs.t[:], 1.0), W=[VRs])
        k.dma("sp", tA.t[:, 0:128], anti_d, W=[tA])
        k.op("dve", lambda e: e.tensor_copy(out=jb.t[:], in_=tA.t[:, 0:128]), R=[tA], W=[jb])
        for jt in (2, 3, 4):
            src = bass.AP(tabp_h, 640 - 128 * jt - 127, [[1, 128], [768, 8], [1, 128]])
            k.dma("sp", tA.t[:].rearrange("p (h i) -> p h i", h=8), src, W=[tA])
            k.op("dve", lambda e: e.tensor_copy(out=xnb.t[:], in_=tA.t[:]), R=[tA], W=[xnb])
            k.op("dve", lambda e: e.tensor_tensor(out=tE[0].t[:], in0=tA.t[:], in1=xnb.t[:], op=ALU.subtract),
                 R=[tA, xnb], W=[tE[0]])
            pa = nextPA()
            for half in range(2):
                hs = slice(half * 512, (half + 1) * 512)
                k.op("pe", lambda e: e.matmul(pa.t[:, hs], lhsT=jb.t[:], rhs=xnb.t[:, hs], start=True, stop=False),
                     R=[jb, xnb], W=[pa], sig=False)
                k.op("pe", lambda e: e.matmul(pa.t[:, hs], lhsT=jb.t[:], rhs=tE[0].t[:, hs], start=False, stop=True),
                     R=[jb, tE[0]], W=[pa], sig=(half == 1))
            k.op("act", lambda e: e.activation(out=EB.t[:, jt, :, :], in_=pa.t[:].rearrange("p (h i) -> p h i", h=8),
                                               func=AF.Exp), R=[pa], W=[EB])
        k.op("dve", lambda e: e.tensor_copy(out=EB.t[:, 1, :, :], in_=EB.t[:, 2, :, :]), R=[EB], W=[EB])
        k.op("dve", lambda e: e.tensor_copy(out=EB.t[:, 0, :, :], in_=EB.t[:, 2, :, :]), R=[EB], W=[EB])
        k.op("dve", lambda e: e.memset(EB.t[0:64, 0, :, 64:128], 0.0), W=[EB])
        k.op("dve", lambda e: e.memset(EB.t[64:128, 4, :, 0:64], 0.0), W=[EB])

        if STOP == "consts":
            k.finish()
            return nc
        k.dma("sp", sk[0:496, :], ck[16:512, :])
        k.dma("sp", sv[0:496, :], cv[16:512, :])
        k.dma("sp", sc[0:14, :], cconv[16:30, :])

        for mt in range(2):
            k.dma("sp", xt[mt].t[:], mem[mt * 128:(mt + 1) * 128, :], W=[xt[mt]])
        for mt in range(2):
            norm_to_T(xt[mt].t[:], xt[mt], 128, D, gfm.t[:, 3, :], W1.t[:, :, mt * 128:(mt + 1) * 128], W1)
        for b in range(2):
            bb, blk = pop("wk%d" % b)
            for oc in range(4):
                pb = fm_proj(bb, blk, W1.t, W1, 256, oc)
                k.op("act", lambda e: e.activation(out=mkT.t[:, b * 4 + oc, :], in_=pb.t[:, 0:256], func=AF.Copy),
                     R=[pb], W=[mkT])
            for mt in range(2):
                pa = nextPA()
                tm_proj(bb, blk, W1.t, W1, mt * 128, 128, pa, 0)
                k.op("act", lambda e: e.activation(out=stage.t[:], in_=pa.t[:, 0:512], func=AF.Copy), R=[pa], W=[stage])
                k.dma("sp", mko[mt * 128:(mt + 1) * 128, b * 512:(b + 1) * 512], stage.t[:], R=[stage])
        for b in range(2):
            bb, blk = pop("wv%d" % b)
            for mt in range(2):
                pa = nextPA()
                tm_proj(bb, blk, W1.t, W1, mt * 128, 128, pa, 0)
                k.op("act", lambda e: e.activation(out=stage2.t[:], in_=pa.t[:, 0:512], func=AF.Copy), R=[pa], W=[stage2])
                k.op("dve", lambda e: e.tensor_copy(out=mv.t[:, mt, b * 512:(b + 1) * 512], in_=stage2.t[:]),
                     R=[stage2], W=[mv])
                k.dma("sp", mvo[mt * 128:(mt + 1) * 128, b * 512:(b + 1) * 512], stage2.t[:], R=[stage2])

        if STOP == "memkv":
            k.finish()
            return nc
        def sample_prep():
            def pipelined(items, bufs, load_fn, compute_fn):
                load_fn(items[0], bufs[0])
                for i, it in enumerate(items):
                    if i + 1 < len(items):
                        load_fn(items[i + 1], bufs[(i + 1) % 2])
                    compute_fn(it, bufs[i % 2])

            itemsA = [("ck", t, 0) for t in range(4)] + [("cmk", mt, half) for mt in range(2) for half in range(2)] \
                + [("cconv", 0, 0)]

            def loadA(it, buf):
                kind, i0, i1 = it
                if kind == "ck":
                    k.dma("sp", buf.t[:], ck[i0 * 128:(i0 + 1) * 128, :], W=[buf])
                elif kind == "cmk":
                    k.dma("sp", buf.t[:], cmk[i0 * 128:(i0 + 1) * 128, i1 * 512:(i1 + 1) * 512], W=[buf])
                else:
                    k.dma("sp", buf.t[0:30, :], cconv, W=[buf])

            def compA(it, buf):
                kind, i0, i1 = it
                pb = PO
                if kind == "cconv":
                    for oc in range(4):
                        k.op("pe", lambda e: e.transpose(pb.t[:, oc * 32:oc * 32 + 30], buf.t[0:30, oc * 128:(oc + 1) * 128],
                                                         idf.t[0:30, 0:30]), R=[buf, idf], W=[pb], sig=(oc == 3))
                    k.op("act", lambda e: e.activation(out=uTs.t[:, :, 0:30],
                                                       in_=pb.t[:, 0:128].rearrange("p (h c) -> p h c", h=4)[:, :, 0:30],
                                                       func=AF.Copy), R=[pb], W=[uTs])
                    return
                for c4 in range(4):
                    k.op("pe", lambda e: e.transpose(pb.t[:, c4 * 128:(c4 + 1) * 128], buf.t[:, c4 * 128:(c4 + 1) * 128],
                                                     idf.t[:]), R=[buf, idf], W=[pb], sig=(c4 == 3))
                if kind == "ck":
                    dst, dbuf = KRs.t[:, :, i0, :], KRs
                else:
                    dst, dbuf = mkTs.t[:, i1 * 4:(i1 + 1) * 4, i0 * 128:(i0 + 1) * 128], mkTs
                k.op("act", lambda e: e.activation(out=dst, in_=pb.t[:].rearrange("p (h c) -> p h c", h=4), func=AF.Copy),
                     R=[pb], W=[dbuf])

            itemsB = [("cv", t, 0) for t in range(4)] + [("cmv", mt, half) for mt in range(2) for half in range(2)]

            def loadB(it, buf):
                kind, i0, i1 = it
                if kind == "cv":
                    k.dma("sp", buf.t[:], cv[i0 * 128:(i0 + 1) * 128, :], W=[buf])
                else:
                    k.dma("sp", buf.t[:], cmv[i0 * 128:(i0 + 1) * 128, i1 * 512:(i1 + 1) * 512], W=[buf])

            def compB(it, buf):
                kind, i0, i1 = it
                if kind == "cv":
                    k.op("dve", lambda e: e.tensor_copy(out=VRs.t[:, i0, :, 0:64],
                                                        in_=buf.t[:].rearrange("p (h d) -> p h d", h=8)), R=[buf], W=[VRs])
                else:
                    k.op("dve", lambda e: e.tensor_copy(out=mvs.t[:, i0, i1 * 512:(i1 + 1) * 512], in_=buf.t[:]),
                         R=[buf], W=[mvs])

            pipelined(itemsB, [tD, tD2[1]], loadB, compB)
            pipelined(itemsA, [a32, a32b], loadA, compA)

        if STOP:
            sample_prep()
        if STOP == "pre":
            k.finish()
            return nc
        def mk_prompt(g):
            G = Grp()
            G.kind = "halo" if g < 0 else "prompt"
            G.n, G.nt, G.r = 512, 4, 128
            G.row0 = (g + 1) * 512
            G.T0 = (g + 1) * 4
            G.xb = xt
            G.x = [xt[i].t[:] for i in range(4)]
            G.W1, G.W2, G.W3, G.W4 = W1, W2, W3, W4
            G.XN = W1
            G.pref = False
            G.acc = acc
            G.uT, G.cb, G.u32 = uT, cb, u32
            G.mkT, G.mv = mkT, mv
            G.last = (g == 3)
            return G

        def mk_sample():
            G = Grp()
            G.kind = "sample"
            G.n, G.nt, G.r = 16, 1, 16
            G.xb = [xst]
            G.x = [xst.t[:]]
            G.W1, G.W2, G.W3, G.W4 = W1s, W2s, W3s, W4s
            G.XN = W1s
            G.pref = False
            G.acc = [accs]
            G.uT, G.cb, G.u32 = uTs, cbs, u32s
            G.mkT, G.mv = mkTs, mvs
            G.last = True
            return G

        def in_norm(G, which):
            for ti in range(G.nt):
                norm_to_T(G.x[ti], G.xb[ti], G.r, D, gfm.t[:, which, :],
                          G.W1.t[:, :, ti * 128:ti * 128 + G.r], G.W1)

        pos = lambda h: (h % 2) * 4 + h // 2

        def att_ktiles(G, pi):
            if G.kind == "sample":
                return [(KRs, VRs, s_, 128) for s_ in range(4)] + [(KRs, VRs, 4, 16)]
            return [(KR, VR, (G.T0 + pi - 4 + jt) % 8, 128) for jt in range(5)]

        def p4buf(G, pi):
            return G.W4

        def att_S_steps(G, pi):
            nq, q0 = G.r, pi * 128
            P4b = p4buf(G, pi)
            P4 = P4b.t[:, 0:5 * 8 * nq].rearrange("p (j h q) -> p j h q", j=5, h=8)
            kts = att_ktiles(G, pi)
            st_ = {}

            def mk(jt):
                kb, vb, slot, kr = kts[jt]

                def mm():
                    pa = nextPA()
                    st_[("pa", jt)] = pa
                    for h in range(8):
                        hp, base = h // 2, (h % 2) * 64
                        col = pos(h) * 128
                        k.op("pe", lambda e: e.matmul(pa.t[0:kr, col:col + nq],
                                                      lhsT=kb.t[base:base + 64, hp, slot, 0:kr],
                                                      rhs=G.W2.t[base:base + 64, hp, q0:q0 + nq],
                                                      start=True, stop=True),
                             R=[kb, G.W2], W=[pa], sig=(h == 7))

                def ex():
                    pa = st_[("pa", jt)]
                    pv = pa.t[0:kr, :].rearrange("p (h q) -> p h q", h=8)[:, :, 0:nq]
                    te_i[0] ^= 1
                    teb = tE[te_i[0]]
                    st_[("te", jt)] = teb
                    tv = teb.t[0:kr, :].rearrange("p (h q) -> p h q", h=8)[:, :, 0:nq]
                    k.op("act", lambda e: e.activation(out=tv, in_=pv, func=AF.Exp, scale=0.125), R=[pa], W=[teb])

                def mu():
                    teb = st_[("te", jt)]
                    tv = teb.t[0:kr, :].rearrange("p (h q) -> p h q", h=8)[:, :, 0:nq]
                    k.op("dve", lambda e: e.tensor_tensor(out=P4[0:kr, jt, :, :], in0=tv,
                                                          in1=EB.t[0:kr, jt, :, 0:nq], op=ALU.mult),
                         R=[teb, EB], W=[P4b])
                return mm, ex, mu
            return [mk(jt) for jt in range(5)]

        def att_S(G, pi):
            for mm, ex, mu in att_S_steps(G, pi):
                mm()
                ex()
                mu()

        def abuf_of(pi):
            return a32 if pi % 2 == 0 else a32b

        def att_PV(G, pi):
            nq = G.r
            ab = abuf_of(pi)
            P4b = p4buf(G, pi)
            P4 = P4b.t[:, 0:5 * 8 * nq].rearrange("p (j h q) -> p j h q", j=5, h=8)
            ktiles = att_ktiles(G, pi)
            for hg in range(2):
                po = PB[hg]
                for hl in range(4):
                    h = hg * 4 + hl
                    for jt, (kb, vb, slot, kr) in enumerate(ktiles):
                        k.op("pe", lambda e: e.matmul(po.t[0:nq, hl * 65:(hl + 1) * 65], lhsT=P4[0:kr, jt, pos(h), :],
                                                      rhs=vb.t[0:kr, slot, h, :], start=(jt == 0), stop=(jt == 4)),
                             R=[P4b, vb], W=[po], sig=(hl == 3 and jt == 4))
                pov = po.t[0:nq, 0:260].rearrange("p (h d) -> p h d", h=4)
                k.op("dve", lambda e: e.reciprocal(out=st_a.t[0:nq, 0:4].unsqueeze(2), in_=pov[:, :, 64:65]),
                     R=[po], W=[st_a])
                k.op("dve", lambda e: e.tensor_tensor(
                    out=ab.t[0:nq, hg * 256:(hg + 1) * 256].rearrange("p (h d) -> p h d", h=4),
                    in0=pov[:, :, 0:64], in1=st_a.t[0:nq, 0:4].unsqueeze(2).to_broadcast([nq, 4, 64]), op=ALU.mult),
                    R=[po, st_a], W=[ab])

        def att_N(G, pi):
            nq, q0 = G.r, pi * 128
            ab = abuf_of(pi)
            norm_to_T(ab.t[0:nq, :], ab, nq, 512, gac.t[:, 0, :], G.W3.t[:, 0:4, q0:q0 + nq], G.W3,
                      xn_buf=xnb, sb_=st_n)

        def att_N_pre(G, pi):
            nq = G.r
            ab = abuf_of(pi)
            norm_pre(ab.t[0:nq, :], ab, nq, 512, xnb, sb_=st_n)

        def att_N_T(G, pi):
            nq, q0 = G.r, pi * 128
            norm_T(xnb, nq, 512, gac.t[:, 0, :], G.W3.t[:, 0:4, q0:q0 + nq], G.W3)

        def conv_mm(G, oc, inject=None):
            n = G.n
            pb = PO
            for w in range(31):
                if inject and w in inject:
                    for fn_ in inject[w]:
                        fn_()
                d = dg[dg_i[0] % NDG]
                dg_i[0] += 1
                k.op("dve", lambda e: e.tensor_scalar(out=d.t[:], in0=idb.t[:], scalar1=wdw.t[:, oc, w:w + 1],
                                                      scalar2=None, op0=ALU.mult), R=[idb, wdw], W=[d])
                k.op("pe", lambda e: e.matmul(pb.t[:, 0:n], lhsT=d.t[:], rhs=G.uT.t[:, oc, w:w + n],
                                              start=(w == 0), stop=(w == 30)), R=[d, G.uT], W=[pb], sig=True)
            k.op("act", lambda e: e.activation(out=G.cb[oc].t[:, 0:n], in_=pb.t[:, 0:n], func=AF.Identity,
                                               bias=bdw.t[:, oc:oc + 1]), R=[pb, bdw], W=[G.cb[oc]])

        def conv_sample(G):
            n = G.n
            for w in range(31):
                for oc in range(4):
                    cbo = G.cb[oc]
                    if w == 0:
                        k.op("dve", lambda e: e.tensor_scalar(out=cbo.t[:, 0:n], in0=G.uT.t[:, oc, 0:n],
                                                              scalar1=wdw.t[:, oc, 0:1], scalar2=bdw.t[:, oc:oc + 1],
                                                              op0=ALU.mult, op1=ALU.add), R=[G.uT, wdw, bdw], W=[cbo])
                    else:
                        k.op("dve", lambda e: e.scalar_tensor_tensor(out=cbo.t[:, 0:n], in0=G.uT.t[:, oc, w:w + n],
                                                                     scalar=wdw.t[:, oc, w:w + 1], in1=cbo.t[:, 0:n],
                                                                     op0=ALU.mult, op1=ALU.add),
                             R=[G.uT, wdw], W=[cbo])

        def attention_and_conv(G):
            if G.kind == "sample":
                att_S(G, 0)
                att_PV(G, 0)
                conv_sample(G)
                att_N(G, 0)
                return
            ocs = [[0], [1], [2], [3]] if G.nt == 4 else [[0, 1, 2, 3]]
            att_S(G, 0)
            for pi in range(G.nt):
                split_n = (pi >= 1 and G.nt == 4 and not STOP)
                if split_n:
                    att_N_pre(G, pi - 1)
                att_PV(G, pi)
                if pi + 1 < G.nt and G.nt == 4 and not STOP:
                    sp = att_S_steps(G, pi + 1)
                    sp[0][0](); sp[0][1](); sp[1][0](); sp[1][1]()
                    inj = {6: [sp[0][2], sp[2][0], sp[2][1]], 12: [sp[1][2], sp[3][0], sp[3][1]],
                           18: [sp[2][2], sp[4][0], sp[4][1]], 24: [sp[3][2]], 29: [sp[4][2]]}
                    conv_mm(G, ocs[pi][0], inject=inj)
                    continue_after = True
                else:
                    continue_after = False
                    if pi + 1 < G.nt:
                        att_S(G, pi + 1)
                for oc in (ocs[pi][1:] if continue_after else ocs[pi]):
                    conv_mm(G, oc)
                if split_n:
                    att_N_T(G, pi - 1)
                elif pi >= 1:
                    att_N(G, pi - 1)
            if G.nt == 4 and not STOP:
                G.defer_attN = True
            else:
                att_N(G, G.nt - 1)

        def conv_post_chain(G, ti, par):
            r = G.r
            c0 = ti * 128
            sc_ = st_c[par]
            tD = tD2[par]
            tC = tC2 if par == 0 else a32
            pa = PB[par]
            for oc in range(4):
                k.op("pe", lambda e: e.transpose(pa.t[0:r, oc * 128:(oc + 1) * 128], G.cb[oc].t[:, c0:c0 + r],
                                                 idf.t[:]), R=[G.cb[oc], idf], W=[pa], sig=(oc == 3))
            ct = pa.t[0:r, 0:512]
            ct3 = ct.rearrange("p (g d) -> p g d", g=8)
            k.op("dve", lambda e: e.reduce_sum(out=sc_.t[0:r, 16:24], in_=ct3, axis=AX.X), R=[pa], W=[sc_])
            k.op("act", lambda e: e.activation(out=tC.t[0:r, :], in_=ct, func=AF.Square), R=[pa], W=[tC])
            k.op("dve", lambda e: e.reduce_sum(out=sc_.t[0:r, 24:32], in_=tC.t[0:r, :].rearrange("p (g d) -> p g d", g=8),
                                               axis=AX.X), R=[tC], W=[sc_])
            k.op("dve", lambda e: e.tensor_scalar(out=sc_.t[0:r, 16:24], in0=sc_.t[0:r, 16:24], scalar1=1.0 / 64,
                                                  scalar2=None, op0=ALU.mult), R=[sc_], W=[sc_])
            k.op("dve", lambda e: e.tensor_tensor(out=sc_.t[0:r, 0:8], in0=sc_.t[0:r, 16:24], in1=sc_.t[0:r, 16:24],
                                                  op=ALU.mult), R=[sc_], W=[sc_])
            k.op("dve", lambda e: e.scalar_tensor_tensor(out=sc_.t[0:r, 24:32], in0=sc_.t[0:r, 24:32], scalar=1.0 / 64,
                                                         in1=sc_.t[0:r, 0:8], op0=ALU.mult, op1=ALU.subtract),
                 R=[sc_], W=[sc_])
            k.op("act", lambda e: e.activation(out=sc_.t[0:r, 24:32], in_=sc_.t[0:r, 24:32], func=AF.Ln,
                                               bias=epsb.t[0:r, 0:1]), R=[sc_, epsb], W=[sc_])
            k.op("act", lambda e: e.activation(out=sc_.t[0:r, 24:32], in_=sc_.t[0:r, 24:32], func=AF.Exp, scale=-0.5),
                 R=[sc_], W=[sc_])
            t3 = tD.t[0:r, :].rearrange("p (g d) -> p g d", g=8)
            k.op("dve", lambda e: e.tensor_tensor(out=t3, in0=ct3, in1=sc_.t[0:r, 16:24].unsqueeze(2).to_broadcast([r, 8, 64]),
                                                  op=ALU.subtract), R=[pa, sc_], W=[tD])
            k.op("dve", lambda e: e.tensor_tensor(out=t3, in0=t3, in1=sc_.t[0:r, 24:32].unsqueeze(2).to_broadcast([r, 8, 64]),
                                                  op=ALU.mult), R=[sc_], W=[tD])
            k.op("dve", lambda e: e.tensor_tensor(out=tD.t[0:r, :], in0=tD.t[0:r, :], in1=lng.t[0:r, :], op=ALU.mult),
                 R=[lng], W=[tD])
            k.op("dve", lambda e: e.tensor_tensor(out=tD.t[0:r, :], in0=tD.t[0:r, :], in1=lnb.t[0:r, :], op=ALU.add),
                 R=[lnb], W=[tD])
            k.op("act", lambda e: e.activation(out=tC.t[0:r, :], in_=tD.t[0:r, :], func=AF.Exp, scale=-1.0),
                 R=[tD], W=[tC])
            k.op("act", lambda e: e.activation(out=tC.t[0:r, :], in_=tC.t[0:r, :], func=AF.Ln, bias=oneb.t[0:r, 0:1]),
                 R=[tC, oneb], W=[tC])
            k.op("act", lambda e: e.activation(out=tC.t[0:r, :], in_=tC.t[0:r, :], func=AF.Exp, scale=-1.0),
                 R=[tC], W=[tC])
            k.op("dve", lambda e: e.tensor_tensor(out=tD.t[0:r, :], in0=tD.t[0:r, :], in1=tC.t[0:r, :], op=ALU.mult),
                 R=[tC], W=[tD])
            norm_pre(tD.t[0:r, :], tD, r, 512, sbf[par], sb_=(st_n2 if par == 0 else st_n3))

        def mixer_tail(Gs):
            load_gbc(0)
            b0, blk0 = pop("wout0")
            b1, blk1 = pop("wout1", keep_prev=True)
            tiles = [(G, ti) for G in Gs for ti in range(G.nt)]
            nt = len(tiles)
            pas = {}

            def U1(i):
                G, ti = tiles[i]
                conv_post_chain(G, ti, i % 2)

            def U2(i):
                G, ti = tiles[i]
                c0 = ti * 128
                norm_T(sbf[i % 2], G.r, 512, gac.t[:, 1, :], G.W3.t[:, 4:8, c0:c0 + G.r], G.W3)
                pa = nextPA()
                tm_proj(b0, blk0, G.W3.t, G.W3, c0, G.r, pa, 0)
                tm_proj(b1, blk1, G.W3.t, G.W3, c0, G.r, pa, 1)
                post_norm_residual(pa.t[0:G.r, :], pa, G.r, gbc0, G.x[ti], G.xb[ti])

            def U3(i):
                G, ti = tiles[i]
                if STOP == "mixer":
                    return
                norm_to_T(G.x[ti], G.xb[ti], G.r, D, gfm.t[:, 1, :], G.W1.t[:, :, ti * 128:ti * 128 + G.r], G.W1)

            def U2a(i):
                G, ti = tiles[i]
                c0 = ti * 128
                norm_T(sbf[i % 2], G.r, 512, gac.t[:, 1, :], G.W3.t[:, 4:8, c0:c0 + G.r], G.W3)

            def U2b(i):
                G, ti = tiles[i]
                c0 = ti * 128
                pa = nextPA()
                tm_proj(b0, blk0, G.W3.t, G.W3, c0, G.r, pa, 0)
                tm_proj(b1, blk1, G.W3.t, G.W3, c0, G.r, pa, 1)
                post_norm_residual(pa.t[0:G.r, :], pa, G.r, gbc0, G.x[ti], G.xb[ti])

            def rest_pair(pair):
                for j in pair:
                    U2b(j)
                for j in pair:
                    U3(j)

            prev = None
            for p0 in range(0, nt, 2):
                pair = [j for j in (p0, p0 + 1) if j < nt]
                fns = [(lambda j=j: U1(j)) for j in pair]
                if prev is not None:
                    fns.append(lambda pr=prev: rest_pair(pr))
                else:
                    for G_ in Gs:
                        if getattr(G_, "defer_attN", False):
                            fns.append(lambda G_=G_: att_N(G_, G_.nt - 1))
                zip_run(k, fns)
                for j in pair:
                    U2a(j)
                prev = pair
            rest_pair(prev)

        def load_gbc(gi):
            k.dma("sp", gbc0.t[:], gpost_d[gi:gi + 1, :].partition_broadcast(128), W=[gbc0])

        def tm_layer(Gs, name, gi, next_norm=None):
            load_gbc(gi)
            b0, blk0 = pop(name + "0")
            b1, blk1 = pop(name + "1", keep_prev=True)
            tiles = [(G, ti) for G in Gs for ti in range(G.nt)]
            pas = {}

            def proj(i):
                G, ti = tiles[i]
                pa = nextPA()
                tm_proj(b0, blk0, G.W3.t, G.W3, ti * 128, G.r, pa, 0)
                tm_proj(b1, blk1, G.W3.t, G.W3, ti * 128, G.r, pa, 1)
                pas[i] = pa

            def chain(i, alt=False):
                G, ti = tiles[i]
                pa = pas[i]
                post_norm_residual(pa.t[0:G.r, :], pa, G.r, gbc0, G.x[ti], G.xb[ti], alt=alt)
                if next_norm is not None:
                    if alt:
                        norm_pre(G.x[ti], G.xb[ti], G.r, D, tE[1], sb_=st_n4)
                    else:
                        norm_pre(G.x[ti], G.xb[ti], G.r, D, xnb)

            def normT(i, alt=False):
                G, ti = tiles[i]
                if next_norm is not None:
                    norm_T(tE[1] if alt else xnb, G.r, D, gfm.t[:, next_norm, :],
                           G.W1.t[:, :, ti * 128:ti * 128 + G.r], G.W1)

            nt_ = len(tiles)
            proj(0)
            if nt_ > 1:
                proj(1)
            i = 0
            while i < nt_:
                if nt_ >= 2 and i == nt_ - 2:
                    zip_run(k, [lambda: chain(nt_ - 2), lambda: chain(nt_ - 1, alt=True)])
                    normT(nt_ - 2)
                    normT(nt_ - 1, alt=True)
                    break
                chain(i)
                if i + 2 < nt_:
                    proj(i + 2)
                normT(i)
                i += 1

        pending = [None]

        def flush_pending():
            if pending[0] is not None:
                fn = pending[0]
                pending[0] = None
                fn()

        def run_pass(Gs, halo=False, side=()):
            defer = pending[0] is not None and all(G.pref for G in Gs if G.kind != "sample")
            if not defer:
                flush_pending()

            def load_x():
                for G in Gs:
                    if G.kind != "sample":
                        for ti in range(4):
                            k.dma("sp", xt[ti].t[:], xh[G.row0 + ti * 128:G.row0 + (ti + 1) * 128, :], W=[xt[ti]])
            for G in Gs:
                if G.kind == "sample" and not G.pref:
                    k.dma("sp", xst.t[:], xs, W=[xst])
            if not defer:
                load_x()
            for G in Gs:
                if not G.pref:
                    in_norm(G, 0)
            def inproj_fn():
                if not halo:
                    bb, blk = pop("q")
                    for G in Gs:
                        for oc in range(4):
                            pb = fm_proj(bb, blk, G.XN.t, G.XN, G.n, oc)
                            k.op("act", lambda e: e.activation(out=G.W2.t[:, oc, 0:G.n], in_=pb.t[:, 0:G.n], func=AF.Copy),
                                 R=[pb], W=[G.W2])
                bb, blk = pop("k")
                for G in Gs:
                    for oc in range(4):
                        pb = fm_proj(bb, blk, G.XN.t, G.XN, G.n, oc)
                        if G.kind == "sample":
                            k.op("act", lambda e: e.activation(out=KRs.t[:, oc, 4, 0:16], in_=pb.t[:, 0:16], func=AF.Copy),
                                 R=[pb], W=[KRs])
                        else:
                            s0 = G.T0 % 8
                            k.op("act", lambda e: e.activation(out=KR.t[:, oc, s0:s0 + 4, :],
                                                               in_=pb.t[:].rearrange("p (s c) -> p s c", s=4), func=AF.Copy),
                                 R=[pb], W=[KR])
                    if G.kind != "halo" and G.last:
                        for ti in range(G.nt):
                            pa = nextPA()
                            tm_proj(bb, blk, G.XN.t, G.XN, ti * 128, G.r, pa, 0)
                            stg = stage if ti % 2 == 0 else tD
                            k.op("act", lambda e: e.activation(out=stg.t[0:G.r, :], in_=pa.t[0:G.r, 0:512], func=AF.Copy),
                                 R=[pa], W=[stg])
                            if G.kind == "sample":
                                k.dma("sp", sk[496:512, :], stg.t[0:16, :], R=[stg])
                            else:
                                k.dma("sp", kout[ti * 128:(ti + 1) * 128, :], stg.t[:], R=[stg])
                bb, blk = pop("v")
                for G in Gs:
                    for ti in range(G.nt):
                        r = G.r
                        pa = nextPA()
                        tm_proj(bb, blk, G.XN.t, G.XN, ti * 128, r, pa, 0)
                        if G.kind == "sample":
                            vdst = VRs.t[0:16, 4, :, 0:64]
                            vbuf = VRs
                        else:
                            slot = (G.T0 + ti) % 8
                            vdst = VR.t[:, slot, :, 0:64]
                            vbuf = VR
                        k.op("act", lambda e: e.activation(out=vdst, in_=pa.t[0:r, 0:512].rearrange("p (h d) -> p h d", h=8),
                                                           func=AF.Copy), R=[pa], W=[vbuf])
                        if G.kind != "sample":
                            T = G.T0 + ti
                            k.op("dve", lambda e: e.tensor_copy(out=VR.t[:, slot, :, 64:65],
                                                                in_=valid.t[:, T:T + 1].unsqueeze(1).to_broadcast([128, 8, 1])),
                                 R=[valid], W=[VR])
                        if G.kind != "halo" and G.last:
                            k.op("dve", lambda e: e.tensor_copy(out=stage2.t[0:r, :], in_=pa.t[0:r, 0:512]), R=[pa], W=[stage2])
                            if G.kind == "sample":
                                k.dma("sp", sv[496:512, :], stage2.t[0:16, :], R=[stage2])
                            else:
                                k.dma("sp", vout[ti * 128:(ti + 1) * 128, :], stage2.t[:], R=[stage2])
                bb, blk = pop("gb")
                for G in Gs:
                    n, hc0 = (128, 384) if G.kind == "halo" else (G.n, 0)
                    for oc in range(4):
                        pb = fm_proj(bb, blk, G.XN.t, G.XN, n, oc, c0=hc0)
                        sgb = G.cb[oc]
                        k.op("act", lambda e: e.activation(out=sgb.t[:, 0:n], in_=pb.t[:, 0:n], func=AF.Exp, scale=-1.0),
                             R=[pb], W=[sgb])
                        k.op("act", lambda e: e.activation(out=sgb.t[:, 0:n], in_=sgb.t[:, 0:n], func=AF.Ln, bias=oneb.t[:, 0:1]),
                             R=[sgb, oneb], W=[sgb])
                        k.op("act", lambda e: e.activation(out=sgb.t[:, 0:n], in_=sgb.t[:, 0:n], func=AF.Exp, scale=-1.0),
                             R=[sgb], W=[sgb])
                bb, blk = pop("ga")
                for G in Gs:
                    n = G.n
                    if G.kind != "sample":
                        k.op("dve", lambda e: e.tensor_copy(out=uT.t[:, :, 0:30], in_=uT.t[:, :, 512:542]), R=[uT], W=[uT])
                    nh, hc0 = (128, 384) if G.kind == "halo" else (n, 0)
                    for oc in range(4):
                        pb = fm_proj(bb, blk, G.XN.t, G.XN, nh, oc, c0=hc0)
                        k.op("dve", lambda e: e.tensor_tensor(out=G.uT.t[:, oc, 30 + hc0:30 + hc0 + nh], in0=pb.t[:, 0:nh],
                                                              in1=G.cb[oc].t[:, 0:nh], op=ALU.mult), R=[pb, G.cb[oc]], W=[G.uT])
                        if G.kind != "halo" and G.last:
                            ntl = 32 if G.kind == "prompt" else 16
                            k.op("dve", lambda e: e.tensor_tensor(out=G.u32.t[:, oc, :], in0=pb.t[:, n - ntl:n],
                                                                  in1=G.cb[oc].t[:, n - ntl:n], op=ALU.mult),
                                 R=[pb, G.cb[oc]], W=[G.u32])
                    if G.kind != "halo" and G.last:
                        nt_ = 32 if G.kind == "prompt" else 16
                        pa = nextPA()
                        for oc in range(4):
                            k.op("pe", lambda e: e.transpose(pa.t[0:nt_, oc * 128:(oc + 1) * 128],
                                                             G.u32.t[:, oc, :], idf.t[:]),
                                 R=[G.u32, idf], W=[pa], sig=(oc == 3))
                        k.op("act", lambda e: e.activation(out=stage.t[0:nt_, :], in_=pa.t[0:nt_, 0:512], func=AF.Copy),
                             R=[pa], W=[stage])
                        if G.kind == "sample":
                            k.dma("sp", sc[14:30, :], stage.t[0:16, :], R=[stage])
                        else:
                            k.dma("sp", uout, stage.t[0:32, :], R=[stage])

            if defer:
                def epilogue_prev():
                    flush_pending()
                    load_x()
                zip_run(k, [inproj_fn, epilogue_prev], weights=[2, 1])
            else:
                inproj_fn()
            if halo or STOP == "inproj":
                return
            for G in Gs:
                attention_and_conv(G)
            mixer_tail(Gs)

            def dump_x():
                for G in Gs:
                    for ti in range(G.nt):
                        if G.kind == "sample":
                            k.dma("sp", ys, xst.t[:], R=[xst])
                        else:
                            r0 = G.row0 - 512 + ti * 128
                            k.dma("sp", y[r0:r0 + 128, :], xt[ti].t[:], R=[xt[ti]])
            if STOP == "mixer":
                dump_x()
                return
            for b in range(2):
                bb, blk = pop("wq%d" % b)
                for G in Gs:
                    for oc in range(4):
                        pb = fm_proj(bb, blk, G.W1.t, G.W1, G.n, oc)
                        k.op("act", lambda e: e.activation(out=G.W2.t[:, b * 4 + oc, 0:G.n], in_=pb.t[:, 0:G.n], func=AF.Copy),
                             R=[pb], W=[G.W2])
            for G in Gs:
                n = G.n
                Pm = G.W4.t[:, 0:8 * n].rearrange("p (h m q) -> p h m q", h=4, m=2)
                Pmh = []
                for h in range(4):
                    al = Buf("Pmh%d" % h, G.W4.t)
                    al.writers = dict(G.W4.writers)
                    al.readers = dict(G.W4.readers)
                    Pmh.append(al)
                for h in range(4):
                    for mt in range(2):
                        pb = nextPB()
                        for dc in range(2):
                            k.op("pe", lambda e: e.matmul(pb.t[:, 0:n], lhsT=G.mkT.t[:, 2 * h + dc, mt * 128:(mt + 1) * 128],
                                                          rhs=G.W2.t[:, 2 * h + dc, 0:n], start=(dc == 0), stop=(dc == 1)),
                                 R=[G.mkT, G.W2], W=[pb], sig=(dc == 1))
                        k.op("act", lambda e: e.activation(out=Pm[:, h, mt, :], in_=pb.t[:, 0:n], func=AF.Exp, scale=1.0 / 16),
                             R=[pb], W=[Pmh[h]])
                den_ps = [(PA[0], 0), (PA[0], 512), (PA[1], 0), (PA[1], 512)]
                rbs = [(tC, tC.t[:, 0:n]), (tD, tD.t[:, 0:n]), (tA, tA.t[:, 0:n]), (tA, tA.t[:, 512:512 + n])]
                for h in range(4):
                    pbuf, c0_ = den_ps[h]
                    for mt in range(2):
                        k.op("pe", lambda e: e.matmul(pbuf.t[:, c0_:c0_ + n], lhsT=onesb.t[:], rhs=Pm[:, h, mt, :],
                                                      start=(mt == 0), stop=(mt == 1)), R=[onesb, Pmh[h]], W=[pbuf],
                             sig=(mt == 1))
                for h in range(4):
                    pbuf, c0_ = den_ps[h]
                    rb, rap = rbs[h]
                    k.op("act", lambda e: e.activation(out=rap, in_=pbuf.t[:, c0_:c0_ + n], func=AF.Ln), R=[pbuf], W=[rb])
                for h in range(4):
                    rb, rap = rbs[h]
                    k.op("act", lambda e: e.activation(out=rap, in_=rap, func=AF.Exp, scale=-1.0), R=[rb], W=[rb])
                for h in range(4):
                    rb, rap = rbs[h]
                    for dc in range(2):
                        pb = nextPB()
                        for mt in range(2):
                            k.op("pe", lambda e: e.matmul(pb.t[:, 0:n],
                                                          lhsT=G.mv.t[:, mt, (2 * h + dc) * 128:(2 * h + dc + 1) * 128],
                                                          rhs=Pm[:, h, mt, :], start=(mt == 0), stop=(mt == 1)),
                                 R=[G.mv, Pmh[h]], W=[pb], sig=(mt == 1))
                        k.op("dve", lambda e: e.tensor_tensor(out=G.W3.t[:, 2 * h + dc, 0:n], in0=pb.t[:, 0:n], in1=rap,
                                                              op=ALU.mult), R=[pb, rb], W=[G.W3])
                mw, mr = {}, {}
                for al in Pmh:
                    for kk, vv in list(al.writers.items()) + list(al.readers.items()):
                        mr[kk] = max(mr.get(kk, 0), vv)
                    for kk, vv in al.writers.items():
                        mw[kk] = max(mw.get(kk, 0), vv)
                G.W4.writers = mw
                G.W4.readers = mr
            tm_layer(Gs, "wo", 1, next_norm=(None if STOP == "mem" else 2))
            if STOP == "mem":
                dump_x()
                return
            def ffn_fn():
                def up(c):
                    bb, blk = pop("up%d" % c)
                    for G in Gs:
                        n = G.n
                        for j in range(4):
                            pb = fm_proj(bb, blk, G.W1.t, G.W1, n, j)
                            k.op("act", lambda e: e.activation(out=tC.t[:, 0:n], in_=pb.t[:, 0:n], func=AF.Relu), R=[pb], W=[tC])
                            k.op("act", lambda e: e.activation(out=G.W2.t[:, (c % 2) * 4 + j, 0:n], in_=tC.t[:, 0:n],
                                                               func=AF.Square), R=[tC], W=[G.W2h[c % 2]])

                def down(c):
                    bb, blk = pop("dn%d" % c)
                    for G in Gs:
                        r = G.r
                        for ti in range(G.nt):
                            pa = nextPA()
                            for half in range(2):
                                for j in range(4):
                                    k.op("pe", lambda e: e.matmul(pa.t[0:r, half * 512:(half + 1) * 512],
                                                                  lhsT=G.W2.t[:, (c % 2) * 4 + j, ti * 128:ti * 128 + r],
                                                                  rhs=blk[:, j, half * 512:(half + 1) * 512],
                                                                  start=(j == 0), stop=(j == 3)),
                                         R=[bb, G.W2h[c % 2]], W=[pa], sig=(j == 3 and half == 1))
                            ab = G.acc[ti]
                            if c == 0:
                                k.op("act", lambda e: e.activation(out=ab.t[0:r, :], in_=pa.t[0:r, :], func=AF.Copy), R=[pa], W=[ab])
                            else:
                                k.op("dve", lambda e: e.tensor_tensor(out=ab.t[0:r, :], in0=ab.t[0:r, :], in1=pa.t[0:r, :],
                                                                      op=ALU.add), R=[pa], W=[ab])

                for G in Gs:
                    G.W2h = []
                    for hh in range(2):
                        al = Buf("W2h%d" % hh, G.W2.t)
                        al.writers = dict(G.W2.writers)
                        al.readers = dict(G.W2.readers)
                        G.W2h.append(al)
                up(0)
                for c in range(8):
                    if c + 1 < 8:
                        up(c + 1)
                    down(c)
                for G in Gs:
                    mw, mr = {}, {}
                    for al in G.W2h:
                        for kk, vv in al.writers.items():
                            mw[kk] = max(mw.get(kk, 0), vv)
                        for kk, vv in al.readers.items():
                            mr[kk] = max(mr.get(kk, 0), vv)
                    for kk, vv in mw.items():
                        mr[kk] = max(mr.get(kk, 0), vv)
                    G.W2.writers = mw
                    G.W2.readers = mr
            zip_run(k, [ffn_fn] + list(side), weights=[6] + [1] * len(side))
            def finish():
                load_gbc(2)
                for G in Gs:
                    for ti in range(G.nt):
                        post_norm_residual(G.acc[ti].t[0:G.r, :], G.acc[ti], G.r, gbc0, G.x[ti], G.xb[ti])
                        if G.kind == "sample":
                            k.dma("sp", ys, xst.t[:], R=[xst])
                        else:
                            r0 = G.row0 - 512 + ti * 128
                            k.dma("sp", y[r0:r0 + 128, :], xt[ti].t[:], R=[xt[ti]])
            pending[0] = finish

        def prefetch_xn(g, with_sample=False):
            def fn():
                row0 = (g + 1) * 512
                for ti in range(4):
                    k.dma("sp", tA.t[:], xh[row0 + ti * 128:row0 + (ti + 1) * 128, :], W=[tA])
                    norm_to_T(tA.t[:], tA, 128, D, gfm.t[:, 0, :], W3.t[:, :, ti * 128:(ti + 1) * 128], W3)
                if with_sample:
                    k.dma("sp", xst.t[:], xs, W=[xst])
                    norm_to_T(xst.t[:], xst, 16, D, gfm.t[:, 0, :], W1s.t[:, :, 0:16], W1s)
            return fn

        run_pass([mk_prompt(-1)], halo=True)
        if STOP == "halo":
            k.finish()
            return nc
        if STOP in ("inproj", "attn", "conv"):
            run_pass([mk_prompt(0)])
            k.finish()
            return nc
        for g in range(3):
            G = mk_prompt(g)
            if g >= 1 and not STOP:
                G.XN, G.pref = W3, True
            side = []
            if not STOP:
                side.append(prefetch_xn(g + 1, with_sample=(g == 2)))
                if g == 2:
                    side.append(sample_prep)
            run_pass([G], side=side)
        G = mk_prompt(3)
        GS = mk_sample()
        if not STOP:
            G.XN, G.pref = W3, True
            GS.pref = True
        run_pass([G, GS])
        flush_pending()
        assert popped[0] == len(sched), (popped[0], len(sched))
        k.finish()
    return nc


_NC = None


def _fm(v, nk):
    return np.ascontiguousarray(np.asarray(v, np.float32).reshape(nk, 128).T)


def kernel(x_prompt, x_sample, mem_prompt, cache_attn_k, cache_attn_v, cache_conv, cache_mem_k, cache_mem_v,
           g_mix_pre, w_in, att_rel_bias, w_dw, b_dw, conv_ln_g, conv_ln_b, g_attn_out, g_conv_out, w_out,
           g_mix_post, g_mem_pre, g_mem_kv, w_mem_q, w_mem_k, w_mem_v, w_mem_o, g_mem_post,
           g_ffn_pre, w_ffn_up, w_ffn_down, g_ffn_post):
    global _NC
    f = lambda a: np.ascontiguousarray(np.asarray(a, dtype=np.float32))
    xp = f(x_prompt)[0]
    xpad = np.concatenate([np.zeros((512, D), np.float32), xp], axis=0)
    gfm = np.stack([_fm(f(g)[0], 8) for g in (g_mix_pre, g_mem_pre, g_ffn_pre, g_mem_kv)], axis=1)
    gac = np.stack([_fm(f(g)[0], 4) for g in (g_attn_out, g_conv_out)], axis=1)
    wdw = np.ascontiguousarray(f(w_dw)[0].T.reshape(4, 128, 31).transpose(1, 0, 2))
    bdw = _fm(f(b_dw)[0], 4)
    gpost = np.stack([f(g_mix_post)[0], f(g_mem_post)[0], f(g_ffn_post)[0]], axis=0)
    tab = f(att_rel_bias)[0]
    tabp = np.pad(tab, ((0, 0), (0, 768 - 257)), mode="edge")
    tabp = np.ascontiguousarray(tabp[[0, 2, 4, 6, 1, 3, 5, 7]])
    common = {
        "mem": f(mem_prompt)[0], "w_in": f(w_in)[0], "w_out": f(w_out)[0], "wq": f(w_mem_q)[0], "wk": f(w_mem_k)[0],
        "wv": f(w_mem_v)[0], "wo": f(w_mem_o)[0], "wup": f(w_ffn_up)[0], "wdn": f(w_ffn_down)[0],
        "gfm": np.ascontiguousarray(gfm), "gac": np.ascontiguousarray(gac), "wdw": wdw, "bdw": bdw,
        "lng": f(conv_ln_g)[0][None, :], "lnb": f(conv_ln_b)[0][None, :], "gpost": np.ascontiguousarray(gpost),
        "tabp": tabp, "ident": np.eye(128, dtype=np.float32),
        "anti": np.ascontiguousarray(np.eye(128, dtype=np.float32)[::-1]),
    }
    in_maps = []
    for c in range(NCORES):
        m = dict(common)
        m["xh"] = np.ascontiguousarray(xpad[c * TOK:c * TOK + TOK + 512])
        m["xs"] = f(x_sample)[c]
        m["ck"] = f(cache_attn_k)[0, c].reshape(512, 512)
        m["cv"] = f(cache_attn_v)[0, c].reshape(512, 512)
        m["cconv"] = f(cache_conv)[0, c]
        m["cmk"] = f(cache_mem_k)[0, c].reshape(256, D)
        m["cmv"] = f(cache_mem_v)[0, c].reshape(256, D)
        v = np.ones((128, 20), np.float32)
        if c == 0:
            v[:, 0:4] = 0.0
        m["valid"] = v
        in_maps.append(m)
    if _NC is None:
        _NC = build()
    res = run_bass_kernel_spmd(_NC, in_maps, core_ids=list(range(NCORES)))
    R = res.results
    y_prompt = np.concatenate([R[c]["y"] for c in range(NCORES)], axis=0)[None]
    y_sample = np.stack([R[c]["ys"] for c in range(NCORES)], axis=0)
    p_k = R[7]["kout"].reshape(1, 1, 512, 8, 64)
    p_v = R[7]["vout"].reshape(1, 1, 512, 8, 64)
    p_c = R[7]["uout"][2:32].reshape(1, 1, 30, 512)
    p_mk = R[0]["mko"].reshape(1, 1, 256, 4, 256)
    p_mv = R[0]["mvo"].reshape(1, 1, 256, 4, 256)
    s_k = np.stack([R[c]["sk"] for c in range(NCORES)], axis=0).reshape(1, 8, 512, 8, 64)
    s_v = np.stack([R[c]["sv"] for c in range(NCORES)], axis=0).reshape(1, 8, 512, 8, 64)
    s_c = np.stack([R[c]["sc"] for c in range(NCORES)], axis=0).reshape(1, 8, 30, 512)
    return (y_prompt.astype(np.float32), y_sample.astype(np.float32), p_k, p_v, p_c, p_mk, p_mv, s_k, s_v, s_c)
```

```python
import numpy as np
from contextlib import ExitStack
import concourse.bass as bass
import concourse.mybir as mybir
from concourse.bass_utils import run_bass_kernel_spmd

F32 = mybir.dt.float32
BF16 = mybir.dt.bfloat16
AF = mybir.ActivationFunctionType
ALU = mybir.AluOpType
AX = mybir.AxisListType

NCORES = 8
S = 16384
TOK = S // NCORES
D = 1024
EPS = 1e-6
NSLOT = 4
import os
STOP = os.environ.get('MK_STOP', '')


class Buf:
    def __init__(self, name, t, excl=False):
        self.name = name
        self.t = t
        self.excl = excl
        self.writers = {}
        self.readers = {}
        self.dsem = None
        self.dcount = 0


class K:
    def __init__(self, nc, st):
        self.nc = nc
        self.st = st
        self.engs = {"pe": nc.tensor, "act": nc.scalar, "dve": nc.vector, "pool": nc.gpsimd, "sp": nc.sync}
        self.sems = {}
        self.cnt = {}
        for e in ("pe", "act", "dve"):
            self.sems[e] = st.enter_context(nc.semaphore("c_" + e))
            self.cnt[e] = 0
        self.seen = {e: {} for e in self.engs}
        self.bufs = []
        self.d2d = st.enter_context(nc.semaphore("d2d"))
        self.d2d_cnt = 0
        self.zip_state = None

    def _yield(self):
        if self.zip_state is not None:
            self.zip_state["yield"]()

    def sb(self, name, shape, dt):
        b = Buf(name, self.st.enter_context(self.nc.sbuf_tensor("s_" + name, shape, dt)))
        self.bufs.append(b)
        return b

    def ps(self, name, shape, dt):
        b = Buf(name, self.st.enter_context(self.nc.psum_tensor("p_" + name, shape, dt)), excl=True)
        self.bufs.append(b)
        return b

    def _waits(self, eng, R, W):
        need = {}

        def add(d):
            for k, v in d.items():
                if need.get(k, 0) < v:
                    need[k] = v
        for b in R:
            add(b.writers)
            if b.excl:
                add(b.readers)
        for b in W:
            add(b.writers)
            add(b.readers)
        E = self.engs[eng]
        for k, v in need.items():
            if k == "pe" and eng == "pe":
                continue
            if self.seen[eng].get(k, 0) < v:
                E.wait_ge(self.sems[k], v)
                self.seen[eng][k] = v

    def op(self, eng, fn, R=(), W=(), sig=True):
        self._waits(eng, R, W)
        ins = fn(self.engs[eng])
        if sig:
            self.cnt[eng] += 1
            ins.then_inc(self.sems[eng], 1)
            val = self.cnt[eng]
        else:
            val = self.cnt[eng] + 1
        for b in W:
            b.writers = {eng: val}
            b.readers = {}
        for b in R:
            if b in W:
                continue
            if b.readers.get(eng, 0) < val:
                b.readers[eng] = val
        if sig:
            self._yield()
        return ins

    def dma(self, q, out, in_, R=(), W=()):
        self._waits(q, R, W)
        ins = self.engs[q].dma_start(out=out, in_=in_)
        b = W[0] if W else (R[0] if R else None)
        if b is None:
            self.d2d_cnt += 16
            ins.then_inc(self.d2d, 16)
            return ins
        if b.dsem is None:
            b.dsem = self.st.enter_context(self.nc.semaphore("d_" + b.name))
            self.sems["d_" + b.name] = b.dsem
        b.dcount += 16
        ins.then_inc(b.dsem, 16)
        key = "d_" + b.name
        for w in W:
            w.writers = {key: b.dcount}
            w.readers = {}
        for r in R:
            r.readers[key] = b.dcount
        return ins

    def finish(self):
        sp = self.engs["sp"]
        for b in self.bufs:
            if b.dsem is not None:
                sp.wait_ge(b.dsem, b.dcount)
        if self.d2d_cnt:
            sp.wait_ge(self.d2d, self.d2d_cnt)
        for e in ("pe", "act", "dve"):
            sp.wait_ge(self.sems[e], self.cnt[e])


class Grp:
    pass


import threading


def zip_run(k, fns, weights=None):
    n = len(fns)
    if n == 1:
        fns[0]()
        return
    assert k.zip_state is None, "nested zip_run not supported"
    sems_ = [threading.Semaphore(0) for _ in range(n)]
    main_sem = threading.Semaphore(0)
    alive = [True] * n
    errs = []

    def pass_baton(i):
        for d in range(1, n + 1):
            j = (i + d) % n
            if alive[j]:
                return j
        return None

    wts = list(weights) if weights else [1] * n
    cnt = [0] * n

    def yield_point():
        i = k.zip_state["cur"]
        cnt[i] += 1
        if cnt[i] < wts[i]:
            return
        cnt[i] = 0
        j = pass_baton(i)
        if j is None or j == i:
            return
        k.zip_state["cur"] = j
        sems_[j].release()
        sems_[i].acquire()

    def runner(i):
        sems_[i].acquire()
        try:
            fns[i]()
        except BaseException as e:
            errs.append(e)
        alive[i] = False
        j = pass_baton(i)
        if j is None:
            main_sem.release()
        else:
            k.zip_state["cur"] = j
            sems_[j].release()

    k.zip_state = {"cur": 0, "yield": yield_point}
    ths = [threading.Thread(target=runner, args=(i,)) for i in range(n)]
    for t in ths:
        t.start()
    sems_[0].release()
    main_sem.acquire()
    for t in ths:
        t.join()
    k.zip_state = None
    if errs:
        raise errs[0]


def build():
    nc = bass.Bass("TRN2", target_bir_lowering=False)
    dt_in = lambda n, s: nc.dram_tensor(n, s, F32, kind="ExternalInput")
    dt_out = lambda n, s: nc.dram_tensor(n, s, F32, kind="ExternalOutput")
    xh = dt_in("xh", [TOK + 512, D]).ap()
    xs = dt_in("xs", [16, D]).ap()
    mem = dt_in("mem", [256, D]).ap()
    ck = dt_in("ck", [512, 512]).ap()
    cv = dt_in("cv", [512, 512]).ap()
    cconv = dt_in("cconv", [30, 512]).ap()
    cmk = dt_in("cmk", [256, D]).ap()
    cmv = dt_in("cmv", [256, D]).ap()
    w_in = dt_in("w_in", [D, 2560]).ap()
    w_out = dt_in("w_out", [D, D]).ap()
    wq = dt_in("wq", [D, D]).ap()
    wk = dt_in("wk", [D, D]).ap()
    wv = dt_in("wv", [D, D]).ap()
    wo = dt_in("wo", [D, D]).ap()
    wup = dt_in("wup", [D, 4096]).ap()
    wdn = dt_in("wdn", [4096, D]).ap()
    gfm_d = dt_in("gfm", [128, 4, 8]).ap()
    gac_d = dt_in("gac", [128, 2, 4]).ap()
    wdw_d = dt_in("wdw", [128, 4, 31]).ap()
    bdw_d = dt_in("bdw", [128, 4]).ap()
    lng_d = dt_in("lng", [1, 512]).ap()
    lnb_d = dt_in("lnb", [1, 512]).ap()
    gpost_d = dt_in("gpost", [3, D]).ap()
    tabp_h = dt_in("tabp", [8, 768])
    valid_d = dt_in("valid", [128, 20]).ap()
    ident_d = dt_in("ident", [128, 128]).ap()
    anti_d = dt_in("anti", [128, 128]).ap()

    y = dt_out("y", [TOK, D]).ap()
    ys = dt_out("ys", [16, D]).ap()
    kout = dt_out("kout", [512, 512]).ap()
    vout = dt_out("vout", [512, 512]).ap()
    uout = dt_out("uout", [32, 512]).ap()
    mko = dt_out("mko", [256, D]).ap()
    mvo = dt_out("mvo", [256, D]).ap()
    sk = dt_out("sk", [512, 512]).ap()
    sv = dt_out("sv", [512, 512]).ap()
    sc = dt_out("sc", [30, 512]).ap()

    with ExitStack() as st:
        k = K(nc, st)
        idf = k.sb("idf", [128, 128], F32)
        idb = k.sb("idb", [128, 128], BF16)
        onesb = k.sb("onesb", [128, 128], BF16)
        jb = k.sb("jb", [128, 128], BF16)
        gfm = k.sb("gfm", [128, 4, 8], F32)
        gac = k.sb("gac", [128, 2, 4], F32)
        wdw = k.sb("wdw", [128, 4, 31], F32)
        bdw = k.sb("bdw", [128, 4], F32)
        lng = k.sb("lng", [128, 512], F32)
        lnb = k.sb("lnb", [128, 512], F32)
        gbc0 = k.sb("gbc0", [128, D], F32)
        valid = k.sb("valid", [128, 20], F32)
        EB = k.sb("EB", [128, 5, 8, 128], BF16)
        ring = [k.sb("ring%d" % i, [128, 4096], BF16) for i in range(NSLOT)]
        KR = k.sb("KR", [128, 4, 8, 128], BF16)
        VR = k.sb("VR", [128, 8, 8, 65], BF16)
        KRs = k.sb("KRs", [128, 4, 5, 128], BF16)
        VRs = k.sb("VRs", [128, 5, 8, 65], BF16)
        mkT = k.sb("mkT", [128, 8, 256], BF16)
        mv = k.sb("mv", [128, 2, D], BF16)
        mkTs = k.sb("mkTs", [128, 8, 256], BF16)
        mvs = k.sb("mvs", [128, 2, D], BF16)
        xt = [k.sb("xt%d" % i, [128, D], F32) for i in range(4)]
        xst = k.sb("xst", [16, D], F32)
        W1 = k.sb("W1", [128, 8, 512], BF16)
        W2 = k.sb("W2", [128, 8, 512], BF16)
        W3 = k.sb("W3", [128, 8, 512], BF16)
        W4 = k.sb("W4", [128, 5120], BF16)
        W1s = k.sb("W1s", [128, 8, 16], BF16)
        W2s = k.sb("W2s", [128, 8, 16], BF16)
        W3s = k.sb("W3s", [128, 8, 16], BF16)
        W4s = k.sb("W4s", [128, 640], BF16)
        acc = [k.sb("acc%d" % i, [128, D], F32) for i in range(4)]
        accs = k.sb("accs", [16, D], F32)
        uT = k.sb("uT", [128, 4, 542], BF16)
        uTs = k.sb("uTs", [128, 4, 46], F32)
        u32 = k.sb("u32", [128, 4, 32], F32)
        u32s = k.sb("u32s", [128, 4, 16], F32)
        NDG = 4
        dg = [k.sb("dg%d" % i, [128, 128], BF16) for i in range(NDG)]
        dg_i = [0]
        cb = [k.sb("cb%d" % i, [128, 512], F32) for i in range(4)]
        cbs = [k.sb("cbs%d" % i, [128, 16], F32) for i in range(4)]
        tA = k.sb("tA", [128, D], F32)
        tE = [k.sb("tE%d" % i, [128, D], BF16) for i in range(2)]
        te_i = [0]
        tC = k.sb("tC", [128, 512], F32)
        tD = k.sb("tD", [128, 512], F32)
        xnb = k.sb("xnb", [128, D], BF16)
        a32 = k.sb("a32", [128, 512], F32)
        a32b = k.sb("a32b", [128, 512], F32)
        st_n = k.sb("st_n", [128, 4], F32)
        st_n2 = k.sb("st_n2", [128, 4], F32)
        st_n3 = k.sb("st_n3", [128, 4], F32)
        st_p = k.sb("st_p", [128, 4], F32)
        st_p2 = k.sb("st_p2", [128, 4], F32)
        st_n4 = k.sb("st_n4", [128, 4], F32)
        st_a = k.sb("st_a", [128, 8], F32)
        st_c = [k.sb("st_c%d" % i, [128, 32], F32) for i in range(2)]
        sbf = [k.sb("sbf%d" % i, [128, 512], BF16) for i in range(2)]
        tD2 = [tD, k.sb("tDb", [128, 512], F32)]
        tC2 = tC
        stage = tD2[1]
        stage2 = a32
        PA = [k.ps("PA%d" % i, [128, D], F32) for i in range(2)]
        PB = [k.ps("PB%d" % i, [128, 512], F32) for i in range(2)]
        PO = k.ps("PO", [128, 512], F32)
        PT = k.ps("PT", [128, 8, 128], BF16)

        pa_i = [0]
        pb_i = [0]

        def nextPA():
            pa_i[0] ^= 1
            return PA[pa_i[0]]

        def nextPB():
            pb_i[0] ^= 1
            return PB[pb_i[0]]

        sched = []

        def kview(w, c0):
            return w.rearrange("(kc p) c -> p kc c", p=128)[:, :, c0:c0 + 512]

        def dview(c):
            return wdn[c * 512:(c + 1) * 512, :].rearrange("(j p) c -> p j c", p=128)

        def sched_pass(kind):
            if kind == "pre":
                for nm, w in (("wk", wk), ("wv", wv)):
                    for b in range(2):
                        sched.append((nm + str(b), "k", kview(w, b * 512)))
                return
            if kind == "halo":
                for nm, c0 in (("k", 512), ("v", 1024), ("gb", 2048), ("ga", 1536)):
                    sched.append((nm, "k", kview(w_in, c0)))
                return
            for nm, c0 in (("q", 0), ("k", 512), ("v", 1024), ("gb", 2048), ("ga", 1536)):
                sched.append((nm, "k", kview(w_in, c0)))
            for nm, w in (("wout", w_out), ("wq", wq), ("wo", wo)):
                if (STOP in ("mixer", "inproj", "attn", "conv") and nm != "wout"):
                    continue
                for b in range(2):
                    sched.append((nm + str(b), "k", kview(w, b * 512)))
            if STOP in ("inproj", "attn", "conv"):
                del sched[-2:]
            if STOP:
                return
            sched.append(("up0", "k", kview(wup, 0)))
            for c in range(8):
                if c + 1 < 8:
                    sched.append(("up%d" % (c + 1), "k", kview(wup, (c + 1) * 512)))
                sched.append(("dn%d" % c, "d", dview(c)))

        sched_pass("pre")
        sched_pass("halo")
        for _ in range(4):
            sched_pass("full")
        issued = [0]
        popped = [0]

        def pop(name, keep_prev=False):
            i = popped[0]
            assert sched[i][0] == name, (sched[i][0], name)
            lim = i + NSLOT - (1 if keep_prev else 0)
            while issued[0] < len(sched) and issued[0] < lim:
                j = issued[0]
                slot = ring[j % NSLOT]
                _, kind, src = sched[j]
                if kind == "k":
                    dst = slot.t[:].rearrange("p (kc c) -> p kc c", kc=8)
                else:
                    dst = slot.t[:].rearrange("p (j c) -> p j c", j=4)
                k.dma("pool", dst, src, W=[slot])
                issued[0] += 1
            popped[0] += 1
            slot = ring[i % NSLOT]
            if sched[i][1] == "k":
                return slot, slot.t[:].rearrange("p (kc c) -> p kc c", kc=8)
            return slot, slot.t[:].rearrange("p (j c) -> p j c", j=4)

        def rstd_from_ss(sb_, r, col, scale):
            k.op("act", lambda e: e.activation(out=sb_.t[0:r, col + 1:col + 2], in_=sb_.t[0:r, col:col + 1],
                                               func=AF.Ln, scale=scale, bias=epsb.t[0:r, 0:1]), R=[sb_, epsb], W=[sb_])
            k.op("act", lambda e: e.activation(out=sb_.t[0:r, col + 1:col + 2], in_=sb_.t[0:r, col + 1:col + 2],
                                               func=AF.Exp, scale=-0.5), R=[sb_], W=[sb_])

        def norm_pre(src_ap, src_buf, r, width, xn_buf, sb_=None):
            sb_ = sb_ or st_n
            k.op("act", lambda e: e.activation(out=xn_buf.t[0:r, 0:width], in_=src_ap, func=AF.Square,
                                               accum_out=sb_.t[0:r, 0:1]), R=[src_buf], W=[xn_buf, sb_])
            rstd_from_ss(sb_, r, 0, 1.0 / width)
            k.op("act", lambda e: e.activation(out=xn_buf.t[0:r, 0:width], in_=src_ap, func=AF.Identity,
                                               scale=sb_.t[0:r, 1:2]), R=[src_buf, sb_], W=[xn_buf])

        def norm_T(xn_buf, r, width, gain_ap, dst_ap, dst_buf):
            nk = width // 128
            for kc in range(nk):
                k.op("pe", lambda e: e.transpose(PT.t[:, kc, 0:r], xn_buf.t[0:r, kc * 128:(kc + 1) * 128],
                                                 idb.t[0:r, 0:r]), R=[xn_buf, idb], W=[PT], sig=(kc == nk - 1))
            k.op("dve", lambda e: e.tensor_tensor(out=dst_ap, in0=PT.t[:, 0:nk, 0:r],
                                                  in1=gain_ap.unsqueeze(2).to_broadcast([128, nk, r]),
                                                  op=ALU.mult), R=[PT, gfm, gac], W=[dst_buf])

        def norm_to_T(src_ap, src_buf, r, width, gain_ap, dst_ap, dst_buf, xn_buf=None, sb_=None):
            xn_buf = xn_buf or xnb
            norm_pre(src_ap, src_buf, r, width, xn_buf, sb_)
            norm_T(xn_buf, r, width, gain_ap, dst_ap, dst_buf)

        def post_norm_residual(ps_ap, ps_buf, r, gb_buf, x_ap, x_buf, alt=False):
            if alt:
                jb_ = tE[1]
                k.op("act", lambda e: e.activation(out=jb_.t[0:r, :], in_=ps_ap, func=AF.Square,
                                                   accum_out=st_p2.t[0:r, 0:1]), R=[ps_buf], W=[jb_, st_p2])
                rstd_from_ss(st_p2, r, 0, 1.0 / D)
                k.op("dve", lambda e: e.scalar_tensor_tensor(out=ps_ap, in0=ps_ap, scalar=st_p2.t[0:r, 1:2],
                                                             in1=gb_buf.t[0:r, :], op0=ALU.mult, op1=ALU.mult),
                     R=[st_p2, gb_buf], W=[ps_buf])
                k.op("dve", lambda e: e.tensor_tensor(out=x_ap, in0=x_ap, in1=ps_ap, op=ALU.add),
                     R=[ps_buf, x_buf], W=[x_buf])
                return
            k.op("act", lambda e: e.activation(out=tA.t[0:r, :], in_=ps_ap, func=AF.Square,
                                               accum_out=st_p.t[0:r, 0:1]), R=[ps_buf], W=[tA, st_p])
            rstd_from_ss(st_p, r, 0, 1.0 / D)
            k.op("dve", lambda e: e.scalar_tensor_tensor(out=tA.t[0:r, :], in0=ps_ap, scalar=st_p.t[0:r, 1:2],
                                                         in1=gb_buf.t[0:r, :], op0=ALU.mult, op1=ALU.mult),
                 R=[ps_buf, st_p, gb_buf], W=[tA])
            k.op("dve", lambda e: e.tensor_tensor(out=x_ap, in0=x_ap, in1=tA.t[0:r, :], op=ALU.add),
                 R=[tA, x_buf], W=[x_buf])

        def fm_proj(blk_buf, blk, src, src_buf, n, oc, c0=0):
            pb = nextPB()
            for kc in range(8):
                k.op("pe", lambda e: e.matmul(pb.t[:, 0:n], lhsT=blk[:, kc, oc * 128:(oc + 1) * 128],
                                              rhs=src[:, kc, c0:c0 + n], start=(kc == 0), stop=(kc == 7)),
                     R=[blk_buf, src_buf], W=[pb], sig=(kc == 7))
            return pb

        def tm_proj(blk_buf, blk, src, src_buf, c0, r, pa, half):
            for kc in range(8):
                k.op("pe", lambda e: e.matmul(pa.t[0:r, half * 512:(half + 1) * 512], lhsT=src[:, kc, c0:c0 + r],
                                              rhs=blk[:, kc, :], start=(kc == 0), stop=(kc == 7)),
                     R=[blk_buf, src_buf], W=[pa], sig=(kc == 7))

        epsb = k.sb("epsb", [128, 1], F32)
        k.op("dve", lambda e: e.memset(epsb.t[:], EPS), W=[epsb])
        oneb = k.sb("oneb", [128, 1], F32)
        k.op("dve", lambda e: e.memset(oneb.t[:], 1.0), W=[oneb])
        k.dma("sp", idf.t[:], ident_d, W=[idf])
        k.dma("sp", gfm.t[:], gfm_d, W=[gfm])
        k.dma("sp", gac.t[:], gac_d, W=[gac])
        k.dma("sp", wdw.t[:], wdw_d, W=[wdw])
        k.dma("sp", bdw.t[:], bdw_d, W=[bdw])
        k.dma("sp", lng.t[:], lng_d.partition_broadcast(128), W=[lng])
        k.dma("sp", lnb.t[:], lnb_d.partition_broadcast(128), W=[lnb])
        k.dma("sp", valid.t[:], valid_d, W=[valid])
        k.op("dve", lambda e: e.tensor_copy(out=idb.t[:], in_=idf.t[:]), R=[idf], W=[idb])
        k.op("dve", lambda e: e.memset(onesb.t[:], 1.0), W=[onesb])
        k.op("dve", lambda e: e.memset(uT.t[:], 0.0), W=[uT])
        k.op("dve", lambda e: e.memset(VRs.t[:], 1.0), W=[VRs])
        k.dma("sp", tA.t[:, 0:128], anti_d, W=[tA])
        k.op("dve", lambda e: e.tensor_copy(out=jb.t[:], in_=tA.t[:, 0:128]), R=[tA], W=[jb])
        for jt in (2, 3, 4):
            src = bass.AP(tabp_h, 640 - 128 * jt - 127, [[1, 128], [768, 8], [1, 128]])
            k.dma("sp", tA.t[:].rearrange("p (h i) -> p h i", h=8), src, W=[tA])
            k.op("dve", lambda e: e.tensor_copy(out=xnb.t[:], in_=tA.t[:]), R=[tA], W=[xnb])
            k.op("dve", lambda e: e.tensor_tensor(out=tE[0].t[:], in0=tA.t[:], in1=xnb.t[:], op=ALU.subtract),
                 R=[tA, xnb], W=[tE[0]])
            pa = nextPA()
            for half in range(2):
                hs = slice(half * 512, (half + 1) * 512)
                k.op("pe", lambda e: e.matmul(pa.t[:, hs], lhsT=jb.t[:], rhs=xnb.t[:, hs], start=True, stop=False),
                     R=[jb, xnb], W=[pa], sig=False)
                k.op("pe", lambda e: e.matmul(pa.t[:, hs], lhsT=jb.t[:], rhs=tE[0].t[:, hs], start=False, stop=True),
                     R=[jb, tE[0]], W=[pa], sig=(half == 1))
            k.op("act", lambda e: e.activation(out=EB.t[:, jt, :, :], in_=pa.t[:].rearrange("p (h i) -> p h i", h=8),
                                               func=AF.Exp), R=[pa], W=[EB])
        k.op("dve", lambda e: e.tensor_copy(out=EB.t[:, 1, :, :], in_=EB.t[:, 2, :, :]), R=[EB], W=[EB])
        k.op("dve", lambda e: e.tensor_copy(out=EB.t[:, 0, :, :], in_=EB.t[:, 2, :, :]), R=[EB], W=[EB])
        k.op("dve", lambda e: e.memset(EB.t[0:64, 0, :, 64:128], 0.0), W=[EB])
        k.op("dve", lambda e: e.memset(EB.t[64:128, 4, :, 0:64], 0.0), W=[EB])

        if STOP == "consts":
            k.finish()
            return nc
        k.dma("sp", sk[0:496, :], ck[16:512, :])
        k.dma("sp", sv[0:496, :], cv[16:512, :])
        k.dma("sp", sc[0:14, :], cconv[16:30, :])

        for mt in range(2):
            k.dma("sp", xt[mt].t[:], mem[mt * 128:(mt + 1) * 128, :], W=[xt[mt]])
        for mt in range(2):
            norm_to_T(xt[mt].t[:], xt[mt], 128, D, gfm.t[:, 3, :], W1.t[:, :, mt * 128:(mt + 1) * 128], W1)
        for b in range(2):
            bb, blk = pop("wk%d" % b)
            for oc in range(4):
                pb = fm_proj(bb, blk, W1.t, W1, 256, oc)
                k.op("act", lambda e: e.activation(out=mkT.t[:, b * 4 + oc, :], in_=pb.t[:, 0:256], func=AF.Copy),
                     R=[pb], W=[mkT])
            for mt in range(2):
                pa = nextPA()
                tm_proj(bb, blk, W1.t, W1, mt * 128, 128, pa, 0)
                k.op("act", lambda e: e.activation(out=stage.t[:], in_=pa.t[:, 0:512], func=AF.Copy), R=[pa], W=[stage])
                k.dma("sp", mko[mt * 128:(mt + 1) * 128, b * 512:(b + 1) * 512], stage.t[:], R=[stage])
        for b in range(2):
            bb, blk = pop("wv%d" % b)
            for mt in range(2):
                pa = nextPA()
                tm_proj(bb, blk, W1.t, W1, mt * 128, 128, pa, 0)
                k.op("act", lambda e: e.activation(out=stage2.t[:], in_=pa.t[:, 0:512], func=AF.Copy), R=[pa], W=[stage2])
                k.op("dve", lambda e: e.tensor_copy(out=mv.t[:, mt, b * 512:(b + 1) * 512], in_=stage2.t[:]),
                     R=[stage2], W=[mv])
                k.dma("sp", mvo[mt * 128:(mt + 1) * 128, b * 512:(b + 1) * 512], stage2.t[:], R=[stage2])

        if STOP == "memkv":
            k.finish()
            return nc
        def sample_prep():
            def pipelined(items, bufs, load_fn, compute_fn):
                load_fn(items[0], bufs[0])
                for i, it in enumerate(items):
                    if i + 1 < len(items):
                        load_fn(items[i + 1], bufs[(i + 1) % 2])
                    compute_fn(it, bufs[i % 2])

            itemsA = [("ck", t, 0) for t in range(4)] + [("cmk", mt, half) for mt in range(2) for half in range(2)] \
                + [("cconv", 0, 0)]

            def loadA(it, buf):
                kind, i0, i1 = it
                if kind == "ck":
                    k.dma("sp", buf.t[:], ck[i0 * 128:(i0 + 1) * 128, :], W=[buf])
                elif kind == "cmk":
                    k.dma("sp", buf.t[:], cmk[i0 * 128:(i0 + 1) * 128, i1 * 512:(i1 + 1) * 512], W=[buf])
                else:
                    k.dma("sp", buf.t[0:30, :], cconv, W=[buf])

            def compA(it, buf):
                kind, i0, i1 = it
                pb = PO
                if kind == "cconv":
                    for oc in range(4):
                        k.op("pe", lambda e: e.transpose(pb.t[:, oc * 32:oc * 32 + 30], buf.t[0:30, oc * 128:(oc + 1) * 128],
                                                         idf.t[0:30, 0:30]), R=[buf, idf], W=[pb], sig=(oc == 3))
                    k.op("act", lambda e: e.activation(out=uTs.t[:, :, 0:30],
                                                       in_=pb.t[:, 0:128].rearrange("p (h c) -> p h c", h=4)[:, :, 0:30],
                                                       func=AF.Copy), R=[pb], W=[uTs])
                    return
                for c4 in range(4):
                    k.op("pe", lambda e: e.transpose(pb.t[:, c4 * 128:(c4 + 1) * 128], buf.t[:, c4 * 128:(c4 + 1) * 128],
                                                     idf.t[:]), R=[buf, idf], W=[pb], sig=(c4 == 3))
                if kind == "ck":
                    dst, dbuf = KRs.t[:, :, i0, :], KRs
                else:
                    dst, dbuf = mkTs.t[:, i1 * 4:(i1 + 1) * 4, i0 * 128:(i0 + 1) * 128], mkTs
                k.op("act", lambda e: e.activation(out=dst, in_=pb.t[:].rearrange("p (h c) -> p h c", h=4), func=AF.Copy),
                     R=[pb], W=[dbuf])

            itemsB = [("cv", t, 0) for t in range(4)] + [("cmv", mt, half) for mt in range(2) for half in range(2)]

            def loadB(it, buf):
                kind, i0, i1 = it
                if kind == "cv":
                    k.dma("sp", buf.t[:], cv[i0 * 128:(i0 + 1) * 128, :], W=[buf])
                else:
                    k.dma("sp", buf.t[:], cmv[i0 * 128:(i0 + 1) * 128, i1 * 512:(i1 + 1) * 512], W=[buf])

            def compB(it, buf):
                kind, i0, i1 = it
                if kind == "cv":
                    k.op("dve", lambda e: e.tensor_copy(out=VRs.t[:, i0, :, 0:64],
                                                        in_=buf.t[:].rearrange("p (h d) -> p h d", h=8)), R=[buf], W=[VRs])
                else:
                    k.op("dve", lambda e: e.tensor_copy(out=mvs.t[:, i0, i1 * 512:(i1 + 1) * 512], in_=buf.t[:]),
                         R=[buf], W=[mvs])

            pipelined(itemsB, [tD, tD2[1]], loadB, compB)
            pipelined(itemsA, [a32, a32b], loadA, compA)

        if STOP:
            sample_prep()
        if STOP == "pre":
            k.finish()
            return nc
        def mk_prompt(g):
            G = Grp()
            G.kind = "halo" if g < 0 else "prompt"
            G.n, G.nt, G.r = 512, 4, 128
            G.row0 = (g + 1) * 512
            G.T0 = (g + 1) * 4
            G.xb = xt
            G.x = [xt[i].t[:] for i in range(4)]
            G.W1, G.W2, G.W3, G.W4 = W1, W2, W3, W4
            G.XN = W1
            G.pref = False
            G.acc = acc
            G.uT, G.cb, G.u32 = uT, cb, u32
            G.mkT, G.mv = mkT, mv
            G.last = (g == 3)
            return G

        def mk_sample():
            G = Grp()
            G.kind = "sample"
            G.n, G.nt, G.r = 16, 1, 16
            G.xb = [xst]
            G.x = [xst.t[:]]
            G.W1, G.W2, G.W3, G.W4 = W1s, W2s, W3s, W4s
            G.XN = W1s
            G.pref = False
            G.acc = [accs]
            G.uT, G.cb, G.u32 = uTs, cbs, u32s
            G.mkT, G.mv = mkTs, mvs
            G.last = True
            return G

        def in_norm(G, which):
            for ti in range(G.nt):
                norm_to_T(G.x[ti], G.xb[ti], G.r, D, gfm.t[:, which, :],
                          G.W1.t[:, :, ti * 128:ti * 128 + G.r], G.W1)

        pos = lambda h: (h % 2) * 4 + h // 2

        def att_ktiles(G, pi):
            if G.kind == "sample":
                return [(KRs, VRs, s_, 128) for s_ in range(4)] + [(KRs, VRs, 4, 16)]
            return [(KR, VR, (G.T0 + pi - 4 + jt) % 8, 128) for jt in range(5)]

        def p4buf(G, pi):
            return G.W4

        def att_S_steps(G, pi):
            nq, q0 = G.r, pi * 128
            P4b = p4buf(G, pi)
            P4 = P4b.t[:, 0:5 * 8 * nq].rearrange("p (j h q) -> p j h q", j=5, h=8)
            kts = att_ktiles(G, pi)
            st_ = {}

            def mk(jt):
                kb, vb, slot, kr = kts[jt]

                def mm():
                    pa = nextPA()
                    st_[("pa", jt)] = pa
                    for h in range(8):
                        hp, base = h // 2, (h % 2) * 64
                        col = pos(h) * 128
                        k.op("pe", lambda e: e.matmul(pa.t[0:kr, col:col + nq],
                                                      lhsT=kb.t[base:base + 64, hp, slot, 0:kr],
                                                      rhs=G.W2.t[base:base + 64, hp, q0:q0 + nq],
                                                      start=True, stop=True),
                             R=[kb, G.W2], W=[pa], sig=(h == 7))

                def ex():
                    pa = st_[("pa", jt)]
                    pv = pa.t[0:kr, :].rearrange("p (h q) -> p h q", h=8)[:, :, 0:nq]
                    te_i[0] ^= 1
                    teb = tE[te_i[0]]
                    st_[("te", jt)] = teb
                    tv = teb.t[0:kr, :].rearrange("p (h q) -> p h q", h=8)[:, :, 0:nq]
                    k.op("act", lambda e: e.activation(out=tv, in_=pv, func=AF.Exp, scale=0.125), R=[pa], W=[teb])

                def mu():
                    teb = st_[("te", jt)]
                    tv = teb.t[0:kr, :].rearrange("p (h q) -> p h q", h=8)[:, :, 0:nq]
                    k.op("dve", lambda e: e.tensor_tensor(out=P4[0:kr, jt, :, :], in0=tv,
                                                          in1=EB.t[0:kr, jt, :, 0:nq], op=ALU.mult),
                         R=[teb, EB], W=[P4b])
                return mm, ex, mu
            return [mk(jt) for jt in range(5)]

        def att_S(G, pi):
            for mm, ex, mu in att_S_steps(G, pi):
                mm()
                ex()
                mu()

        def abuf_of(pi):
            return a32 if pi % 2 == 0 else a32b

        def att_PV(G, pi):
            nq = G.r
            ab = abuf_of(pi)
            P4b = p4buf(G, pi)
            P4 = P4b.t[:, 0:5 * 8 * nq].rearrange("p (j h q) -> p j h q", j=5, h=8)
            ktiles = att_ktiles(G, pi)
            for hg in range(2):
                po = PB[hg]
                for hl in range(4):
                    h = hg * 4 + hl
                    for jt, (kb, vb, slot, kr) in enumerate(ktiles):
                        k.op("pe", lambda e: e.matmul(po.t[0:nq, hl * 65:(hl + 1) * 65], lhsT=P4[0:kr, jt, pos(h), :],
                                                      rhs=vb.t[0:kr, slot, h, :], start=(jt == 0), stop=(jt == 4)),
                             R=[P4b, vb], W=[po], sig=(hl == 3 and jt == 4))
                pov = po.t[0:nq, 0:260].rearrange("p (h d) -> p h d", h=4)
                k.op("dve", lambda e: e.reciprocal(out=st_a.t[0:nq, 0:4].unsqueeze(2), in_=pov[:, :, 64:65]),
                     R=[po], W=[st_a])
                k.op("dve", lambda e: e.tensor_tensor(
                    out=ab.t[0:nq, hg * 256:(hg + 1) * 256].rearrange("p (h d) -> p h d", h=4),
                    in0=pov[:, :, 0:64], in1=st_a.t[0:nq, 0:4].unsqueeze(2).to_broadcast([nq, 4, 64]), op=ALU.mult),
                    R=[po, st_a], W=[ab])

        def att_N(G, pi):
            nq, q0 = G.r, pi * 128
            ab = abuf_of(pi)
            norm_to_T(ab.t[0:nq, :], ab, nq, 512, gac.t[:, 0, :], G.W3.t[:, 0:4, q0:q0 + nq], G.W3,
                      xn_buf=xnb, sb_=st_n)

        def att_N_pre(G, pi):
            nq = G.r
            ab = abuf_of(pi)
            norm_pre(ab.t[0:nq, :], ab, nq, 512, xnb, sb_=st_n)

        def att_N_T(G, pi):
            nq, q0 = G.r, pi * 128
            norm_T(xnb, nq, 512, gac.t[:, 0, :], G.W3.t[:, 0:4, q0:q0 + nq], G.W3)

        def conv_mm(G, oc, inject=None):
            n = G.n
            pb = PO
            for w in range(31):
                if inject and w in inject:
                    for fn_ in inject[w]:
                        fn_()
                d = dg[dg_i[0] % NDG]
                dg_i[0] += 1
                k.op("dve", lambda e: e.tensor_scalar(out=d.t[:], in0=idb.t[:], scalar1=wdw.t[:, oc, w:w + 1],
                                                      scalar2=None, op0=ALU.mult), R=[idb, wdw], W=[d])
                k.op("pe", lambda e: e.matmul(pb.t[:, 0:n], lhsT=d.t[:], rhs=G.uT.t[:, oc, w:w + n],
                                              start=(w == 0), stop=(w == 30)), R=[d, G.uT], W=[pb], sig=True)
            k.op("act", lambda e: e.activation(out=G.cb[oc].t[:, 0:n], in_=pb.t[:, 0:n], func=AF.Identity,
                                               bias=bdw.t[:, oc:oc + 1]), R=[pb, bdw], W=[G.cb[oc]])

        def conv_sample(G):
            n = G.n
            for w in range(31):
                for oc in range(4):
                    cbo = G.cb[oc]
                    if w == 0:
                        k.op("dve", lambda e: e.tensor_scalar(out=cbo.t[:, 0:n], in0=G.uT.t[:, oc, 0:n],
                                                              scalar1=wdw.t[:, oc, 0:1], scalar2=bdw.t[:, oc:oc + 1],
                                                              op0=ALU.mult, op1=ALU.add), R=[G.uT, wdw, bdw], W=[cbo])
                    else:
                        k.op("dve", lambda e: e.scalar_tensor_tensor(out=cbo.t[:, 0:n], in0=G.uT.t[:, oc, w:w + n],
                                                                     scalar=wdw.t[:, oc, w:w + 1], in1=cbo.t[:, 0:n],
                                                                     op0=ALU.mult, op1=ALU.add),
                             R=[G.uT, wdw], W=[cbo])

        def attention_and_conv(G):
            if G.kind == "sample":
                att_S(G, 0)
                att_PV(G, 0)
                conv_sample(G)
                att_N(G, 0)
                return
            ocs = [[0], [1], [2], [3]] if G.nt == 4 else [[0, 1, 2, 3]]
            att_S(G, 0)
            for pi in range(G.nt):
                split_n = (pi >= 1 and G.nt == 4 and not STOP)
                if split_n:
                    att_N_pre(G, pi - 1)
                last_first = (G.nt == 4 and pi == G.nt - 1 and not STOP)
                if last_first:
                    for oc in ocs[pi]:
                        conv_mm(G, oc)
                att_PV(G, pi)
                if pi + 1 < G.nt and G.nt == 4 and not STOP:
                    sp = att_S_steps(G, pi + 1)
                    sp[0][0](); sp[0][1](); sp[1][0](); sp[1][1]()
                    inj = {6: [sp[0][2], sp[2][0], sp[2][1]], 12: [sp[1][2], sp[3][0], sp[3][1]],
                           18: [sp[2][2], sp[4][0], sp[4][1]], 24: [sp[3][2]], 29: [sp[4][2]]}
                    conv_mm(G, ocs[pi][0], inject=inj)
                    continue_after = True
                else:
                    continue_after = False
                    if pi + 1 < G.nt:
                        att_S(G, pi + 1)
                for oc in ([] if last_first else (ocs[pi][1:] if continue_after else ocs[pi])):
                    conv_mm(G, oc)
                if split_n:
                    att_N_T(G, pi - 1)
                elif pi >= 1:
                    att_N(G, pi - 1)
            if G.nt == 4 and not STOP:
                G.defer_attN = True
            else:
                att_N(G, G.nt - 1)

        def conv_post_chain(G, ti, par):
            r = G.r
            c0 = ti * 128
            sc_ = st_c[par]
            tD = tD2[par]
            tC = tC2 if par == 0 else a32
            pa = PB[par]
            for oc in range(4):
                k.op("pe", lambda e: e.transpose(pa.t[0:r, oc * 128:(oc + 1) * 128], G.cb[oc].t[:, c0:c0 + r],
                                                 idf.t[:]), R=[G.cb[oc], idf], W=[pa], sig=(oc == 3))
            ct = pa.t[0:r, 0:512]
            ct3 = ct.rearrange("p (g d) -> p g d", g=8)
            k.op("dve", lambda e: e.reduce_sum(out=sc_.t[0:r, 16:24], in_=ct3, axis=AX.X), R=[pa], W=[sc_])
            k.op("act", lambda e: e.activation(out=tC.t[0:r, :], in_=ct, func=AF.Square), R=[pa], W=[tC])
            k.op("dve", lambda e: e.reduce_sum(out=sc_.t[0:r, 24:32], in_=tC.t[0:r, :].rearrange("p (g d) -> p g d", g=8),
                                               axis=AX.X), R=[tC], W=[sc_])
            k.op("dve", lambda e: e.tensor_scalar(out=sc_.t[0:r, 16:24], in0=sc_.t[0:r, 16:24], scalar1=1.0 / 64,
                                                  scalar2=None, op0=ALU.mult), R=[sc_], W=[sc_])
            k.op("dve", lambda e: e.tensor_tensor(out=sc_.t[0:r, 0:8], in0=sc_.t[0:r, 16:24], in1=sc_.t[0:r, 16:24],
                                                  op=ALU.mult), R=[sc_], W=[sc_])
            k.op("dve", lambda e: e.scalar_tensor_tensor(out=sc_.t[0:r, 24:32], in0=sc_.t[0:r, 24:32], scalar=1.0 / 64,
                                                         in1=sc_.t[0:r, 0:8], op0=ALU.mult, op1=ALU.subtract),
                 R=[sc_], W=[sc_])
            k.op("act", lambda e: e.activation(out=sc_.t[0:r, 24:32], in_=sc_.t[0:r, 24:32], func=AF.Ln,
                                               bias=epsb.t[0:r, 0:1]), R=[sc_, epsb], W=[sc_])
            k.op("act", lambda e: e.activation(out=sc_.t[0:r, 24:32], in_=sc_.t[0:r, 24:32], func=AF.Exp, scale=-0.5),
                 R=[sc_], W=[sc_])
            t3 = tD.t[0:r, :].rearrange("p (g d) -> p g d", g=8)
            k.op("dve", lambda e: e.tensor_tensor(out=t3, in0=ct3, in1=sc_.t[0:r, 16:24].unsqueeze(2).to_broadcast([r, 8, 64]),
                                                  op=ALU.subtract), R=[pa, sc_], W=[tD])
            k.op("dve", lambda e: e.tensor_tensor(out=t3, in0=t3, in1=sc_.t[0:r, 24:32].unsqueeze(2).to_broadcast([r, 8, 64]),
                                                  op=ALU.mult), R=[sc_], W=[tD])
            k.op("dve", lambda e: e.tensor_tensor(out=tD.t[0:r, :], in0=tD.t[0:r, :], in1=lng.t[0:r, :], op=ALU.mult),
                 R=[lng], W=[tD])
            k.op("dve", lambda e: e.tensor_tensor(out=tD.t[0:r, :], in0=tD.t[0:r, :], in1=lnb.t[0:r, :], op=ALU.add),
                 R=[lnb], W=[tD])
            k.op("act", lambda e: e.activation(out=tC.t[0:r, :], in_=tD.t[0:r, :], func=AF.Exp, scale=-1.0),
                 R=[tD], W=[tC])
            k.op("act", lambda e: e.activation(out=tC.t[0:r, :], in_=tC.t[0:r, :], func=AF.Ln, bias=oneb.t[0:r, 0:1]),
                 R=[tC, oneb], W=[tC])
            k.op("act", lambda e: e.activation(out=tC.t[0:r, :], in_=tC.t[0:r, :], func=AF.Exp, scale=-1.0),
                 R=[tC], W=[tC])
            k.op("dve", lambda e: e.tensor_tensor(out=tD.t[0:r, :], in0=tD.t[0:r, :], in1=tC.t[0:r, :], op=ALU.mult),
                 R=[tC], W=[tD])
            norm_pre(tD.t[0:r, :], tD, r, 512, sbf[par], sb_=(st_n2 if par == 0 else st_n3))

        def mixer_tail(Gs):
            load_gbc(0)
            b0, blk0 = pop("wout0")
            b1, blk1 = pop("wout1", keep_prev=True)
            tiles = [(G, ti) for G in Gs for ti in range(G.nt)]
            nt = len(tiles)
            pas = {}

            def U1(i):
                G, ti = tiles[i]
                conv_post_chain(G, ti, i % 2)

            def U2(i):
                G, ti = tiles[i]
                c0 = ti * 128
                norm_T(sbf[i % 2], G.r, 512, gac.t[:, 1, :], G.W3.t[:, 4:8, c0:c0 + G.r], G.W3)
                pa = nextPA()
                tm_proj(b0, blk0, G.W3.t, G.W3, c0, G.r, pa, 0)
                tm_proj(b1, blk1, G.W3.t, G.W3, c0, G.r, pa, 1)
                post_norm_residual(pa.t[0:G.r, :], pa, G.r, gbc0, G.x[ti], G.xb[ti])

            def U3(i):
                G, ti = tiles[i]
                if STOP == "mixer":
                    return
                norm_to_T(G.x[ti], G.xb[ti], G.r, D, gfm.t[:, 1, :], G.W1.t[:, :, ti * 128:ti * 128 + G.r], G.W1)

            def U2a(i):
                G, ti = tiles[i]
                c0 = ti * 128
                norm_T(sbf[i % 2], G.r, 512, gac.t[:, 1, :], G.W3.t[:, 4:8, c0:c0 + G.r], G.W3)

            def U2b(i):
                G, ti = tiles[i]
                c0 = ti * 128
                pa = nextPA()
                tm_proj(b0, blk0, G.W3.t, G.W3, c0, G.r, pa, 0)
                tm_proj(b1, blk1, G.W3.t, G.W3, c0, G.r, pa, 1)
                post_norm_residual(pa.t[0:G.r, :], pa, G.r, gbc0, G.x[ti], G.xb[ti])

            def rest_pair(pair):
                for j in pair:
                    U2b(j)
                for j in pair:
                    U3(j)

            prev = None
            for p0 in range(0, nt, 2):
                pair = [j for j in (p0, p0 + 1) if j < nt]
                fns = [(lambda j=j: U1(j)) for j in pair]
                if prev is not None:
                    fns.append(lambda pr=prev: rest_pair(pr))
                else:
                    for G_ in Gs:
                        if getattr(G_, "defer_attN", False):
                            fns.append(lambda G_=G_: att_N(G_, G_.nt - 1))
                zip_run(k, fns)
                for j in pair:
                    U2a(j)
                prev = pair
            rest_pair(prev)

        def load_gbc(gi):
            k.dma("sp", gbc0.t[:], gpost_d[gi:gi + 1, :].partition_broadcast(128), W=[gbc0])

        def tm_layer(Gs, name, gi, next_norm=None):
            load_gbc(gi)
            b0, blk0 = pop(name + "0")
            b1, blk1 = pop(name + "1", keep_prev=True)
            tiles = [(G, ti) for G in Gs for ti in range(G.nt)]
            pas = {}

            def proj(i):
                G, ti = tiles[i]
                pa = nextPA()
                tm_proj(b0, blk0, G.W3.t, G.W3, ti * 128, G.r, pa, 0)
                tm_proj(b1, blk1, G.W3.t, G.W3, ti * 128, G.r, pa, 1)
                pas[i] = pa

            def chain(i, alt=False):
                G, ti = tiles[i]
                pa = pas[i]
                post_norm_residual(pa.t[0:G.r, :], pa, G.r, gbc0, G.x[ti], G.xb[ti], alt=alt)
                if next_norm is not None:
                    if alt:
                        norm_pre(G.x[ti], G.xb[ti], G.r, D, tE[1], sb_=st_n4)
                    else:
                        norm_pre(G.x[ti], G.xb[ti], G.r, D, xnb)

            def normT(i, alt=False):
                G, ti = tiles[i]
                if next_norm is not None:
                    norm_T(tE[1] if alt else xnb, G.r, D, gfm.t[:, next_norm, :],
                           G.W1.t[:, :, ti * 128:ti * 128 + G.r], G.W1)

            nt_ = len(tiles)
            proj(0)
            if nt_ > 1:
                proj(1)
            i = 0
            while i < nt_:
                if nt_ >= 2 and i == nt_ - 2:
                    zip_run(k, [lambda: chain(nt_ - 2), lambda: chain(nt_ - 1, alt=True)])
                    normT(nt_ - 2)
                    normT(nt_ - 1, alt=True)
                    break
                chain(i)
                if i + 2 < nt_:
                    proj(i + 2)
                normT(i)
                i += 1

        pending = [None]

        def flush_pending():
            if pending[0] is not None:
                fn = pending[0]
                pending[0] = None
                fn()

        def run_pass(Gs, halo=False, side=()):
            defer = pending[0] is not None and all(G.pref for G in Gs if G.kind != "sample")
            if not defer:
                flush_pending()

            def load_x():
                for G in Gs:
                    if G.kind != "sample":
                        for ti in range(4):
                            k.dma("sp", xt[ti].t[:], xh[G.row0 + ti * 128:G.row0 + (ti + 1) * 128, :], W=[xt[ti]])
            for G in Gs:
                if G.kind == "sample" and not G.pref:
                    k.dma("sp", xst.t[:], xs, W=[xst])
            if not defer:
                load_x()
            for G in Gs:
                if not G.pref:
                    in_norm(G, 0)
            def inproj_fn():
                if not halo:
                    bb, blk = pop("q")
                    for G in Gs:
                        for oc in range(4):
                            pb = fm_proj(bb, blk, G.XN.t, G.XN, G.n, oc)
                            k.op("act", lambda e: e.activation(out=G.W2.t[:, oc, 0:G.n], in_=pb.t[:, 0:G.n], func=AF.Copy),
                                 R=[pb], W=[G.W2])
                bb, blk = pop("k")
                for G in Gs:
                    for oc in range(4):
                        pb = fm_proj(bb, blk, G.XN.t, G.XN, G.n, oc)
                        if G.kind == "sample":
                            k.op("act", lambda e: e.activation(out=KRs.t[:, oc, 4, 0:16], in_=pb.t[:, 0:16], func=AF.Copy),
                                 R=[pb], W=[KRs])
                        else:
                            s0 = G.T0 % 8
                            k.op("act", lambda e: e.activation(out=KR.t[:, oc, s0:s0 + 4, :],
                                                               in_=pb.t[:].rearrange("p (s c) -> p s c", s=4), func=AF.Copy),
                                 R=[pb], W=[KR])
                    if G.kind != "halo" and G.last:
                        for ti in range(G.nt):
                            pa = nextPA()
                            tm_proj(bb, blk, G.XN.t, G.XN, ti * 128, G.r, pa, 0)
                            stg = stage if ti % 2 == 0 else tD
                            k.op("act", lambda e: e.activation(out=stg.t[0:G.r, :], in_=pa.t[0:G.r, 0:512], func=AF.Copy),
                                 R=[pa], W=[stg])
                            if G.kind == "sample":
                                k.dma("sp", sk[496:512, :], stg.t[0:16, :], R=[stg])
                            else:
                                k.dma("sp", kout[ti * 128:(ti + 1) * 128, :], stg.t[:], R=[stg])
                bb, blk = pop("v")
                for G in Gs:
                    for ti in range(G.nt):
                        r = G.r
                        pa = nextPA()
                        tm_proj(bb, blk, G.XN.t, G.XN, ti * 128, r, pa, 0)
                        if G.kind == "sample":
                            vdst = VRs.t[0:16, 4, :, 0:64]
                            vbuf = VRs
                        else:
                            slot = (G.T0 + ti) % 8
                            vdst = VR.t[:, slot, :, 0:64]
                            vbuf = VR
                        k.op("act", lambda e: e.activation(out=vdst, in_=pa.t[0:r, 0:512].rearrange("p (h d) -> p h d", h=8),
                                                           func=AF.Copy), R=[pa], W=[vbuf])
                        if G.kind != "sample":
                            T = G.T0 + ti
                            k.op("dve", lambda e: e.tensor_copy(out=VR.t[:, slot, :, 64:65],
                                                                in_=valid.t[:, T:T + 1].unsqueeze(1).to_broadcast([128, 8, 1])),
                                 R=[valid], W=[VR])
                        if G.kind != "halo" and G.last:
                            k.op("dve", lambda e: e.tensor_copy(out=stage2.t[0:r, :], in_=pa.t[0:r, 0:512]), R=[pa], W=[stage2])
                            if G.kind == "sample":
                                k.dma("sp", sv[496:512, :], stage2.t[0:16, :], R=[stage2])
                            else:
                                k.dma("sp", vout[ti * 128:(ti + 1) * 128, :], stage2.t[:], R=[stage2])
                bb, blk = pop("gb")
                for G in Gs:
                    n, hc0 = (128, 384) if G.kind == "halo" else (G.n, 0)
                    for oc in range(4):
                        pb = fm_proj(bb, blk, G.XN.t, G.XN, n, oc, c0=hc0)
                        sgb = G.cb[oc]
                        k.op("act", lambda e: e.activation(out=sgb.t[:, 0:n], in_=pb.t[:, 0:n], func=AF.Exp, scale=-1.0),
                             R=[pb], W=[sgb])
                        k.op("act", lambda e: e.activation(out=sgb.t[:, 0:n], in_=sgb.t[:, 0:n], func=AF.Ln, bias=oneb.t[:, 0:1]),
                             R=[sgb, oneb], W=[sgb])
                        k.op("act", lambda e: e.activation(out=sgb.t[:, 0:n], in_=sgb.t[:, 0:n], func=AF.Exp, scale=-1.0),
                             R=[sgb], W=[sgb])
                bb, blk = pop("ga")
                for G in Gs:
                    n = G.n
                    if G.kind != "sample":
                        k.op("dve", lambda e: e.tensor_copy(out=uT.t[:, :, 0:30], in_=uT.t[:, :, 512:542]), R=[uT], W=[uT])
                    nh, hc0 = (128, 384) if G.kind == "halo" else (n, 0)
                    for oc in range(4):
                        pb = fm_proj(bb, blk, G.XN.t, G.XN, nh, oc, c0=hc0)
                        k.op("dve", lambda e: e.tensor_tensor(out=G.uT.t[:, oc, 30 + hc0:30 + hc0 + nh], in0=pb.t[:, 0:nh],
                                                              in1=G.cb[oc].t[:, 0:nh], op=ALU.mult), R=[pb, G.cb[oc]], W=[G.uT])
                        if G.kind != "halo" and G.last:
                            ntl = 32 if G.kind == "prompt" else 16
                            k.op("dve", lambda e: e.tensor_tensor(out=G.u32.t[:, oc, :], in0=pb.t[:, n - ntl:n],
                                                                  in1=G.cb[oc].t[:, n - ntl:n], op=ALU.mult),
                                 R=[pb, G.cb[oc]], W=[G.u32])
                    if G.kind != "halo" and G.last:
                        nt_ = 32 if G.kind == "prompt" else 16
                        pa = nextPA()
                        for oc in range(4):
                            k.op("pe", lambda e: e.transpose(pa.t[0:nt_, oc * 128:(oc + 1) * 128],
                                                             G.u32.t[:, oc, :], idf.t[:]),
                                 R=[G.u32, idf], W=[pa], sig=(oc == 3))
                        k.op("act", lambda e: e.activation(out=stage.t[0:nt_, :], in_=pa.t[0:nt_, 0:512], func=AF.Copy),
                             R=[pa], W=[stage])
                        if G.kind == "sample":
                            k.dma("sp", sc[14:30, :], stage.t[0:16, :], R=[stage])
                        else:
                            k.dma("sp", uout, stage.t[0:32, :], R=[stage])

            if defer:
                def epilogue_prev():
                    flush_pending()
                    load_x()
                zip_run(k, [inproj_fn, epilogue_prev], weights=[2, 1])
            else:
                inproj_fn()
            if halo or STOP == "inproj":
                return
            for G in Gs:
                attention_and_conv(G)
            mixer_tail(Gs)

            def dump_x():
                for G in Gs:
                    for ti in range(G.nt):
                        if G.kind == "sample":
                            k.dma("sp", ys, xst.t[:], R=[xst])
                        else:
                            r0 = G.row0 - 512 + ti * 128
                            k.dma("sp", y[r0:r0 + 128, :], xt[ti].t[:], R=[xt[ti]])
            if STOP == "mixer":
                dump_x()
                return
            for b in range(2):
                bb, blk = pop("wq%d" % b)
                for G in Gs:
                    for oc in range(4):
                        pb = fm_proj(bb, blk, G.W1.t, G.W1, G.n, oc)
                        k.op("act", lambda e: e.activation(out=G.W2.t[:, b * 4 + oc, 0:G.n], in_=pb.t[:, 0:G.n], func=AF.Copy),
                             R=[pb], W=[G.W2])
            for G in Gs:
                n = G.n
                Pm = G.W4.t[:, 0:8 * n].rearrange("p (h m q) -> p h m q", h=4, m=2)
                Pmh = []
                for h in range(4):
                    al = Buf("Pmh%d" % h, G.W4.t)
                    al.writers = dict(G.W4.writers)
                    al.readers = dict(G.W4.readers)
                    Pmh.append(al)
                for h in range(4):
                    for mt in range(2):
                        pb = nextPB()
                        for dc in range(2):
                            k.op("pe", lambda e: e.matmul(pb.t[:, 0:n], lhsT=G.mkT.t[:, 2 * h + dc, mt * 128:(mt + 1) * 128],
                                                          rhs=G.W2.t[:, 2 * h + dc, 0:n], start=(dc == 0), stop=(dc == 1)),
                                 R=[G.mkT, G.W2], W=[pb], sig=(dc == 1))
                        k.op("act", lambda e: e.activation(out=Pm[:, h, mt, :], in_=pb.t[:, 0:n], func=AF.Exp, scale=1.0 / 16),
                             R=[pb], W=[Pmh[h]])
                den_ps = [(PA[0], 0), (PA[0], 512), (PA[1], 0), (PA[1], 512)]
                rbs = [(tC, tC.t[:, 0:n]), (tD, tD.t[:, 0:n]), (tA, tA.t[:, 0:n]), (tA, tA.t[:, 512:512 + n])]
                for h in range(4):
                    pbuf, c0_ = den_ps[h]
                    for mt in range(2):
                        k.op("pe", lambda e: e.matmul(pbuf.t[:, c0_:c0_ + n], lhsT=onesb.t[:], rhs=Pm[:, h, mt, :],
                                                      start=(mt == 0), stop=(mt == 1)), R=[onesb, Pmh[h]], W=[pbuf],
                             sig=(mt == 1))
                for h in range(4):
                    pbuf, c0_ = den_ps[h]
                    rb, rap = rbs[h]
                    k.op("act", lambda e: e.activation(out=rap, in_=pbuf.t[:, c0_:c0_ + n], func=AF.Ln), R=[pbuf], W=[rb])
                for h in range(4):
                    rb, rap = rbs[h]
                    k.op("act", lambda e: e.activation(out=rap, in_=rap, func=AF.Exp, scale=-1.0), R=[rb], W=[rb])
                for h in range(4):
                    rb, rap = rbs[h]
                    for dc in range(2):
                        pb = nextPB()
                        for mt in range(2):
                            k.op("pe", lambda e: e.matmul(pb.t[:, 0:n],
                                                          lhsT=G.mv.t[:, mt, (2 * h + dc) * 128:(2 * h + dc + 1) * 128],
                                                          rhs=Pm[:, h, mt, :], start=(mt == 0), stop=(mt == 1)),
                                 R=[G.mv, Pmh[h]], W=[pb], sig=(mt == 1))
                        k.op("dve", lambda e: e.tensor_tensor(out=G.W3.t[:, 2 * h + dc, 0:n], in0=pb.t[:, 0:n], in1=rap,
                                                              op=ALU.mult), R=[pb, rb], W=[G.W3])
                mw, mr = {}, {}
                for al in Pmh:
                    for kk, vv in list(al.writers.items()) + list(al.readers.items()):
                        mr[kk] = max(mr.get(kk, 0), vv)
                    for kk, vv in al.writers.items():
                        mw[kk] = max(mw.get(kk, 0), vv)
                G.W4.writers = mw
                G.W4.readers = mr
            tm_layer(Gs, "wo", 1, next_norm=(None if STOP == "mem" else 2))
            if STOP == "mem":
                dump_x()
                return
            def ffn_fn():
                def up(c):
                    bb, blk = pop("up%d" % c)
                    for G in Gs:
                        n = G.n
                        for j in range(4):
                            pb = fm_proj(bb, blk, G.W1.t, G.W1, n, j)
                            k.op("act", lambda e: e.activation(out=tC.t[:, 0:n], in_=pb.t[:, 0:n], func=AF.Relu), R=[pb], W=[tC])
                            k.op("act", lambda e: e.activation(out=G.W2.t[:, (c % 2) * 4 + j, 0:n], in_=tC.t[:, 0:n],
                                                               func=AF.Square), R=[tC], W=[G.W2h[c % 2]])

                def down(c):
                    bb, blk = pop("dn%d" % c)
                    for G in Gs:
                        r = G.r
                        for ti in range(G.nt):
                            pa = nextPA()
                            for half in range(2):
                                for j in range(4):
                                    k.op("pe", lambda e: e.matmul(pa.t[0:r, half * 512:(half + 1) * 512],
                                                                  lhsT=G.W2.t[:, (c % 2) * 4 + j, ti * 128:ti * 128 + r],
                                                                  rhs=blk[:, j, half * 512:(half + 1) * 512],
                                                                  start=(j == 0), stop=(j == 3)),
                                         R=[bb, G.W2h[c % 2]], W=[pa], sig=(j == 3 and half == 1))
                            ab = G.acc[ti]
                            if c == 0:
                                k.op("act", lambda e: e.activation(out=ab.t[0:r, :], in_=pa.t[0:r, :], func=AF.Copy), R=[pa], W=[ab])
                            else:
                                k.op("dve", lambda e: e.tensor_tensor(out=ab.t[0:r, :], in0=ab.t[0:r, :], in1=pa.t[0:r, :],
                                                                      op=ALU.add), R=[pa], W=[ab])

                for G in Gs:
                    G.W2h = []
                    for hh in range(2):
                        al = Buf("W2h%d" % hh, G.W2.t)
                        al.writers = dict(G.W2.writers)
                        al.readers = dict(G.W2.readers)
                        G.W2h.append(al)
                up(0)
                for c in range(8):
                    if c + 1 < 8:
                        up(c + 1)
                    down(c)
                for G in Gs:
                    mw, mr = {}, {}
                    for al in G.W2h:
                        for kk, vv in al.writers.items():
                            mw[kk] = max(mw.get(kk, 0), vv)
                        for kk, vv in al.readers.items():
                            mr[kk] = max(mr.get(kk, 0), vv)
                    for kk, vv in mw.items():
                        mr[kk] = max(mr.get(kk, 0), vv)
                    G.W2.writers = mw
                    G.W2.readers = mr
            zip_run(k, [ffn_fn] + list(side), weights=[6] + [1] * len(side))
            def finish():
                load_gbc(2)
                for G in Gs:
                    for ti in range(G.nt):
                        post_norm_residual(G.acc[ti].t[0:G.r, :], G.acc[ti], G.r, gbc0, G.x[ti], G.xb[ti])
                        if G.kind == "sample":
                            k.dma("sp", ys, xst.t[:], R=[xst])
                        else:
                            r0 = G.row0 - 512 + ti * 128
                            k.dma("sp", y[r0:r0 + 128, :], xt[ti].t[:], R=[xt[ti]])
            pending[0] = finish

        def prefetch_xn(g, with_sample=False):
            def fn():
                row0 = (g + 1) * 512
                for ti in range(4):
                    k.dma("sp", tA.t[:], xh[row0 + ti * 128:row0 + (ti + 1) * 128, :], W=[tA])
                    norm_to_T(tA.t[:], tA, 128, D, gfm.t[:, 0, :], W3.t[:, :, ti * 128:(ti + 1) * 128], W3)
                if with_sample:
                    k.dma("sp", xst.t[:], xs, W=[xst])
                    norm_to_T(xst.t[:], xst, 16, D, gfm.t[:, 0, :], W1s.t[:, :, 0:16], W1s)
            return fn

        run_pass([mk_prompt(-1)], halo=True)
        if STOP == "halo":
            k.finish()
            return nc
        if STOP in ("inproj", "attn", "conv"):
            run_pass([mk_prompt(0)])
            k.finish()
            return nc
        for g in range(3):
            G = mk_prompt(g)
            if g >= 1 and not STOP:
                G.XN, G.pref = W3, True
            side = []
            if not STOP:
                side.append(prefetch_xn(g + 1, with_sample=(g == 2)))
                if g == 2:
                    side.append(sample_prep)
            run_pass([G], side=side)
        G = mk_prompt(3)
        GS = mk_sample()
        if not STOP:
            G.XN, G.pref = W3, True
            GS.pref = True
        run_pass([G, GS])
        flush_pending()
        assert popped[0] == len(sched), (popped[0], len(sched))
        k.finish()
    return nc


_NC = None


def _fm(v, nk):
    return np.ascontiguousarray(np.asarray(v, np.float32).reshape(nk, 128).T)


def kernel(x_prompt, x_sample, mem_prompt, cache_attn_k, cache_attn_v, cache_conv, cache_mem_k, cache_mem_v,
           g_mix_pre, w_in, att_rel_bias, w_dw, b_dw, conv_ln_g, conv_ln_b, g_attn_out, g_conv_out, w_out,
           g_mix_post, g_mem_pre, g_mem_kv, w_mem_q, w_mem_k, w_mem_v, w_mem_o, g_mem_post,
           g_ffn_pre, w_ffn_up, w_ffn_down, g_ffn_post):
    global _NC
    f = lambda a: np.ascontiguousarray(np.asarray(a, dtype=np.float32))
    xp = f(x_prompt)[0]
    xpad = np.concatenate([np.zeros((512, D), np.float32), xp], axis=0)
    gfm = np.stack([_fm(f(g)[0], 8) for g in (g_mix_pre, g_mem_pre, g_ffn_pre, g_mem_kv)], axis=1)
    gac = np.stack([_fm(f(g)[0], 4) for g in (g_attn_out, g_conv_out)], axis=1)
    wdw = np.ascontiguousarray(f(w_dw)[0].T.reshape(4, 128, 31).transpose(1, 0, 2))
    bdw = _fm(f(b_dw)[0], 4)
    gpost = np.stack([f(g_mix_post)[0], f(g_mem_post)[0], f(g_ffn_post)[0]], axis=0)
    tab = f(att_rel_bias)[0]
    tabp = np.pad(tab, ((0, 0), (0, 768 - 257)), mode="edge")
    tabp = np.ascontiguousarray(tabp[[0, 2, 4, 6, 1, 3, 5, 7]])
    common = {
        "mem": f(mem_prompt)[0], "w_in": f(w_in)[0], "w_out": f(w_out)[0], "wq": f(w_mem_q)[0], "wk": f(w_mem_k)[0],
        "wv": f(w_mem_v)[0], "wo": f(w_mem_o)[0], "wup": f(w_ffn_up)[0], "wdn": f(w_ffn_down)[0],
        "gfm": np.ascontiguousarray(gfm), "gac": np.ascontiguousarray(gac), "wdw": wdw, "bdw": bdw,
        "lng": f(conv_ln_g)[0][None, :], "lnb": f(conv_ln_b)[0][None, :], "gpost": np.ascontiguousarray(gpost),
        "tabp": tabp, "ident": np.eye(128, dtype=np.float32),
        "anti": np.ascontiguousarray(np.eye(128, dtype=np.float32)[::-1]),
    }
    in_maps = []
    for c in range(NCORES):
        m = dict(common)
        m["xh"] = np.ascontiguousarray(xpad[c * TOK:c * TOK + TOK + 512])
        m["xs"] = f(x_sample)[c]
        m["ck"] = f(cache_attn_k)[0, c].reshape(512, 512)
        m["cv"] = f(cache_attn_v)[0, c].reshape(512, 512)
        m["cconv"] = f(cache_conv)[0, c]
        m["cmk"] = f(cache_mem_k)[0, c].reshape(256, D)
        m["cmv"] = f(cache_mem_v)[0, c].reshape(256, D)
        v = np.ones((128, 20), np.float32)
        if c == 0:
            v[:, 0:4] = 0.0
        m["valid"] = v
        in_maps.append(m)
    if _NC is None:
        _NC = build()
    res = run_bass_kernel_spmd(_NC, in_maps, core_ids=list(range(NCORES)))
    R = res.results
    y_prompt = np.concatenate([R[c]["y"] for c in range(NCORES)], axis=0)[None]
    y_sample = np.stack([R[c]["ys"] for c in range(NCORES)], axis=0)
    p_k = R[7]["kout"].reshape(1, 1, 512, 8, 64)
    p_v = R[7]["vout"].reshape(1, 1, 512, 8, 64)
    p_c = R[7]["uout"][2:32].reshape(1, 1, 30, 512)
    p_mk = R[0]["mko"].reshape(1, 1, 256, 4, 256)
    p_mv = R[0]["mvo"].reshape(1, 1, 256, 4, 256)
    s_k = np.stack([R[c]["sk"] for c in range(NCORES)], axis=0).reshape(1, 8, 512, 8, 64)
    s_v = np.stack([R[c]["sv"] for c in range(NCORES)], axis=0).reshape(1, 8, 512, 8, 64)
    s_c = np.stack([R[c]["sc"] for c in range(NCORES)], axis=0).reshape(1, 8, 30, 512)
    return (y_prompt.astype(np.float32), y_sample.astype(np.float32), p_k, p_v, p_c, p_mk, p_mv, s_k, s_v, s_c)
```
